# Optimizing a Trainium2 kernel written in Bass

```python
import math
import jax
import jax.numpy as jnp
from jax import lax
import numpy as np

D_MODEL = 2048
BATCH = 4
SEQ = 4096
DEPTH = 2

HEAD_DIM = 128
N_HEADS = D_MODEL // HEAD_DIM
N_HEADS_A = N_HEADS // 2
N_HEADS_B = N_HEADS - N_HEADS_A
DILATED_BRANCHES = ((128, 1), (512, 4), (2048, 16))
MOBA_BLOCK = 256
MOBA_TOPK = 3
MOBA_QCHUNK = 32
SB_QBLOCK = 128
REL_BUCKETS = 32
REL_MAX_DIST = 2048
D_FF = 5632
FFN_HALF = 0.5
RMS_EPS = 1e-6
N_EVEN = (DEPTH + 1) // 2
N_ODD = DEPTH // 2

kernel_name = 'hybrid_dilated_moba_stickbreaking_macaron'


def rms_norm(x, g):
    xf = x.astype(jnp.float32)
    y = xf * lax.rsqrt(jnp.mean(xf * xf, axis=-1, keepdims=True) + RMS_EPS)
    return (y * g.astype(jnp.float32)).astype(x.dtype)


def swiglu(x, w_gate, w_up, w_down):
    return (jax.nn.silu(x @ w_gate) * (x @ w_up)) @ w_down


def rel_bucket(dist):
    max_exact = REL_BUCKETS // 2
    d = jnp.maximum(dist, 0)
    df = jnp.maximum(d, 1).astype(jnp.float32)
    large = max_exact + (jnp.log(df / max_exact) / math.log(REL_MAX_DIST / max_exact)
                         * (REL_BUCKETS - max_exact)).astype(jnp.int32)
    large = jnp.minimum(large, REL_BUCKETS - 1)
    return jnp.where(d < max_exact, d, large)


def dilated_branch(q, k, v, tab, window, dilation):
    Bb, H, S, hd = q.shape
    W = window // dilation
    L = S // dilation
    nb = -(-L // W)
    Lp = nb * W

    def to_sub(t):
        t = t.reshape(Bb, H, L, dilation, hd).transpose(0, 1, 3, 2, 4)
        return jnp.pad(t, ((0, 0), (0, 0), (0, 0), (0, Lp - L), (0, 0)))

    def band(t):
        tp = jnp.pad(to_sub(t), ((0, 0), (0, 0), (0, 0), (W, 0), (0, 0))).reshape(Bb, H, dilation, nb + 1, W, hd)
        return jnp.concatenate([tp[:, :, :, :-1], tp[:, :, :, 1:]], axis=4)

    qs = to_sub(q).reshape(Bb, H, dilation, nb, W, hd)
    kb, vb = band(k), band(v)
    i = jnp.arange(W)[:, None]
    j = jnp.arange(2 * W)[None, :]
    rel = W + i - j
    blk_start = jnp.arange(nb)[:, None, None] * W
    valid = (rel >= 0) & (rel <= W) & (blk_start - W + j >= 0)
    bias = tab[:, rel_bucket(rel * dilation)]
    s = jnp.einsum('bhrnqd,bhrnkd->bhrnqk', qs, kb) * (HEAD_DIM ** -0.5) + bias[:, None, None]
    s = jnp.where(valid, s, -jnp.inf)
    m = jnp.max(s, axis=-1, keepdims=True)
    p = jnp.exp(s - m)
    den = jnp.sum(p, axis=-1)
    o = jnp.einsum('bhrnqk,bhrnkd->bhrnqd', p, vb) / den[..., None]
    lse = m[..., 0] + jnp.log(den)

    def from_sub(t, tail):
        t = t.reshape(Bb, H, dilation, Lp, *tail)[:, :, :, :L]
        return jnp.moveaxis(t, 2, 3).reshape(Bb, H, S, *tail)

    return from_sub(o, (hd,)), from_sub(lse, ())


def dilated_mixture(q, k, v, tab):
    results = [dilated_branch(q, k, v, tab, w, r) for (w, r) in DILATED_BRANCHES]
    outs = jnp.stack([o for (o, _) in results])
    lses = jnp.stack([l for (_, l) in results])
    alpha = jax.nn.softmax(lses, axis=0)
    return jnp.sum(alpha[..., None] * outs, axis=0)


def moba_attention(q, k, v, tab):
    Bb, H, S, hd = q.shape
    nblk = -(-S // MOBA_BLOCK)
    Sp = nblk * MOBA_BLOCK
    pad = ((0, 0), (0, 0), (0, Sp - S), (0, 0))
    kp, vp = jnp.pad(k, pad), jnp.pad(v, pad)
    kb = kp.reshape(Bb, H, nblk, MOBA_BLOCK, hd)
    vb = vp.reshape(Bb, H, nblk, MOBA_BLOCK, hd)
    own = jnp.arange(S) // MOBA_BLOCK
    gate = jnp.einsum('bhsd,bhnd->bhsn', q, jnp.mean(kb, axis=3))
    gate = jnp.where(jnp.arange(nblk)[None, :] < own[:, None], gate, -jnp.inf)
    n_sel = min(MOBA_TOPK, nblk)
    _, sel = lax.top_k(gate, n_sel)
    sel_ok = sel < own[:, None]
    nC = S // MOBA_QCHUNK

    def chunks(t):
        return jnp.moveaxis(t.reshape(Bb, H, nC, MOBA_QCHUNK, *t.shape[3:]), 2, 0)

    bi = jnp.arange(Bb)[:, None, None, None]
    hi = jnp.arange(H)[None, :, None, None]
    hi5 = jnp.arange(H)[None, :, None, None, None]
    offs = jnp.arange(MOBA_BLOCK)
    scale = HEAD_DIM ** -0.5

    def chunk_attn(args):
        qc, selc, okc, c = args
        qpos = c * MOBA_QCHUNK + jnp.arange(MOBA_QCHUNK)
        own_start = (c * MOBA_QCHUNK // MOBA_BLOCK) * MOBA_BLOCK
        k_sel = kb[bi, hi, selc]
        v_sel = vb[bi, hi, selc]
        k_own = lax.dynamic_slice_in_dim(kp, own_start, MOBA_BLOCK, axis=2)
        v_own = lax.dynamic_slice_in_dim(vp, own_start, MOBA_BLOCK, axis=2)
        dist_sel = qpos[:, None, None] - (selc[..., None] * MOBA_BLOCK + offs)
        s_sel = jnp.einsum('bhqd,bhqnkd->bhqnk', qc, k_sel) * scale + tab[hi5, rel_bucket(dist_sel)]
        s_sel = jnp.where(okc[..., None], s_sel, -jnp.inf)
        dist_own = qpos[:, None] - (own_start + offs)[None, :]
        s_own = jnp.einsum('bhqd,bhkd->bhqk', qc, k_own) * scale + tab[:, rel_bucket(dist_own)]
        s_own = jnp.where(dist_own >= 0, s_own, -jnp.inf)
        n_k = n_sel * MOBA_BLOCK
        p = jax.nn.softmax(jnp.concatenate([s_sel.reshape(Bb, H, MOBA_QCHUNK, n_k), s_own], axis=-1), axis=-1)
        p_sel = p[..., :n_k].reshape(Bb, H, MOBA_QCHUNK, n_sel, MOBA_BLOCK)
        return (jnp.einsum('bhqnk,bhqnkd->bhqd', p_sel, v_sel)
                + jnp.einsum('bhqk,bhkd->bhqd', p[..., n_k:], v_own))

    out = lax.map(chunk_attn, (chunks(q), chunks(sel), chunks(sel_ok), jnp.arange(nC)))
    return jnp.moveaxis(out, 0, 2).reshape(Bb, H, S, hd)


def stick_breaking_attention(q, k, v):
    Bb, H, S, hd = q.shape
    nQ = S // SB_QBLOCK
    kpos = jnp.arange(S)
    scale = HEAD_DIM ** -0.5

    def block(args):
        qc, c = args
        qpos = c * SB_QBLOCK + jnp.arange(SB_QBLOCK)
        past = kpos[None, :] < qpos[:, None]
        z = jnp.einsum('bhqd,bhkd->bhqk', qc, k) * scale
        log_keep = jnp.where(past, jax.nn.log_sigmoid(-z), 0.0)
        between = lax.cumsum(log_keep, axis=3, reverse=True) - log_keep
        log_w = jnp.where(past, jax.nn.log_sigmoid(z) + between, -jnp.inf)
        return jnp.einsum('bhqk,bhkd->bhqd', jnp.exp(log_w), v)

    qb = jnp.moveaxis(q.reshape(Bb, H, nQ, SB_QBLOCK, hd), 2, 0)
    out = lax.map(block, (qb, jnp.arange(nQ)))
    return jnp.moveaxis(out, 0, 2).reshape(Bb, H, S, hd)


def split_qkv(u, w_qkv):
    Bb, S, _ = u.shape
    qkv = (u @ w_qkv).reshape(Bb, S, 3, N_HEADS, HEAD_DIM).astype(jnp.float32)
    return [qkv[:, :, j].transpose(0, 2, 1, 3) for j in range(3)]


def merge_heads(o, w_out, dtype):
    Bb, H, S, hd = o.shape
    return o.transpose(0, 2, 1, 3).reshape(Bb, S, H * hd).astype(dtype) @ w_out


def mix_even(u, w_qkv, w_out, rel_bias):
    q, k, v = split_qkv(u, w_qkv)
    tab = rel_bias.astype(jnp.float32).T
    a = slice(0, N_HEADS_A)
    b = slice(N_HEADS_A, N_HEADS)
    o_a = dilated_mixture(q[:, a], k[:, a], v[:, a], tab[a])
    o_b = moba_attention(q[:, b], k[:, b], v[:, b], tab[b])
    return merge_heads(jnp.concatenate([o_a, o_b], axis=1), w_out, u.dtype)


def mix_odd(u, w_qkv, w_out):
    q, k, v = split_qkv(u, w_qkv)
    return merge_heads(stick_breaking_attention(q, k, v), w_out, u.dtype)


def setup_inputs(seed: int = 0) -> dict:
    key = jax.random.key(seed)
    ks = jax.random.split(key, 12)
    D = D_MODEL
    nrm = jax.random.normal
    x = nrm(ks[0], (BATCH, SEQ, D), jnp.float32)
    ln_gains = 1.0 + 0.02 * nrm(ks[1], (DEPTH, 3, D), jnp.float32)
    ffn_w_gate = nrm(ks[2], (DEPTH, 2, D, D_FF), jnp.float32) * D ** -0.5
    ffn_w_up = nrm(ks[3], (DEPTH, 2, D, D_FF), jnp.float32) * D ** -0.5
    ffn_w_down = nrm(ks[4], (DEPTH, 2, D_FF, D), jnp.float32) * D_FF ** -0.5
    w_qkv_even = nrm(ks[5], (N_EVEN, D, 3 * D), jnp.float32) * D ** -0.5
    w_out_even = nrm(ks[6], (N_EVEN, D, D), jnp.float32) * D ** -0.5
    w_qkv_odd = nrm(ks[7], (N_ODD, D, 3 * D), jnp.float32) * D ** -0.5
    w_out_odd = nrm(ks[8], (N_ODD, D, D), jnp.float32) * D ** -0.5
    rel_bias = 0.5 * nrm(ks[9], (REL_BUCKETS, N_HEADS), jnp.float32)
    final_gain = 1.0 + 0.02 * nrm(ks[10], (D,), jnp.float32)
    return {'x': x, 'ln_gains': ln_gains, 'ffn_w_gate': ffn_w_gate, 'ffn_w_up': ffn_w_up,
            'ffn_w_down': ffn_w_down, 'w_qkv_even': w_qkv_even, 'w_out_even': w_out_even,
            'w_qkv_odd': w_qkv_odd, 'w_out_odd': w_out_odd, 'rel_bias': rel_bias,
            'final_gain': final_gain}


def reference(x, ln_gains, ffn_w_gate, ffn_w_up, ffn_w_down, w_qkv_even, w_out_even,
              w_qkv_odd, w_out_odd, rel_bias, final_gain):
    h = x
    for i in range(DEPTH):
        h = h + FFN_HALF * swiglu(rms_norm(h, ln_gains[i, 0]), ffn_w_gate[i, 0], ffn_w_up[i, 0], ffn_w_down[i, 0])
        u = rms_norm(h, ln_gains[i, 1])
        if i % 2 == 0:
            h = h + mix_even(u, w_qkv_even[i // 2], w_out_even[i // 2], rel_bias)
        else:
            h = h + mix_odd(u, w_qkv_odd[i // 2], w_out_odd[i // 2])
        h = h + FFN_HALF * swiglu(rms_norm(h, ln_gains[i, 2]), ffn_w_gate[i, 1], ffn_w_up[i, 1], ffn_w_down[i, 1])
    return rms_norm(h, final_gain)
```

```python
import math
from contextlib import ExitStack

import numpy as np
import ml_dtypes
import concourse.bass as bass
import concourse.mybir as mybir
from concourse.bass_utils import run_bass_kernel_spmd

F32 = mybir.dt.float32
BF16 = mybir.dt.bfloat16
ALU = mybir.AluOpType
AF = mybir.ActivationFunctionType
AX = mybir.AxisListType

D = 2048
DC = 16
DFF = 5632
FC = 44
SEQ = 4096
NTOK = 2048
NT = 4
TS = 512
HD = 128
NH = 16
SCALE = HD ** -0.5
EPS = 1e-6
XL = 3584
GW = 3456
BIG = 30000.0
CSTW = 256 + 16 * 128 + 128


class Dom:
    def __init__(self, name, sem, step):
        self.name, self.sem, self.step = name, sem, step
        self.total = 0
        self.last_op = None
        self.pending = []


class Buf:
    __slots__ = ("name", "last_w", "readers", "dma", "multi", "writers")

    def __init__(self, name, dma=None, multi=False):
        self.name, self.last_w, self.readers, self.dma = name, None, {}, dma
        self.multi, self.writers = multi, {}


class Op:
    __slots__ = ("eng", "dom", "emit", "deps", "signal", "sigval", "seq", "emitted", "next_sig", "tag")

    def __init__(self, eng, dom, emit, seq):
        self.eng, self.dom, self.emit, self.seq = eng, dom, emit, seq
        self.deps, self.signal, self.sigval = [], False, None
        self.emitted, self.next_sig = False, None


ENGS = ("pe", "act", "dve", "pool", "sp")


class Sched:
    def __init__(self, nc, n_dma_sems=96, n_cc_sems=0):
        self.nc = nc
        self.cdom = {e: Dom(e, nc.alloc_semaphore("sem_" + e), 1)
                     for e in ("pe", "act", "dve", "pool")}
        self.dma_pool = [Dom("dma%d" % i, nc.alloc_semaphore("semd%d" % i), 16)
                         for i in range(n_dma_sems)]
        n_sw = 9
        self.sw_pool, self.hw_pool = self.dma_pool[:n_sw], self.dma_pool[n_sw:]
        self.dma_next = {"sw": 0, "hw": 0}
        self.ops = {e: [] for e in ENGS}
        self.waited = {e: {} for e in ENGS}
        self.barrier_deps = {e: [] for e in ENGS}
        self.cc_doms = [Dom("cc%d" % i, nc.alloc_semaphore("semcc%d" % i), 1) for i in range(n_cc_sems)]
        self.cc_next = 0
        self.all_doms = list(self.cdom.values()) + self.dma_pool + self.cc_doms
        self.seq = 0
        self.n_inst = 0

    tag = ""

    def buf(self, name, dma=False, multi=False):
        d = None
        if dma:
            kind = "sw" if dma == "sw" else "hw"
            pool = self.sw_pool if kind == "sw" else self.hw_pool
            d = pool[self.dma_next[kind] % len(pool)]
            self.dma_next[kind] += 1
        return Buf(name, d, multi)

    @staticmethod
    def _res(o):
        if o.emitted and not o.signal:
            o = o.next_sig
        return o

    def _adddep(self, deps, o):
        o = self._res(o)
        cur = deps.get(o.dom)
        if cur is None or cur.seq < o.seq:
            deps[o.dom] = o

    def _record(self, eng, dom, emit, reads, writes, is_dma):
        self.seq += 1
        op = Op(eng, dom, emit, self.seq)
        op.tag = self.tag
        deps = {}
        for b in reads:
            if b.multi:
                for o in b.writers.values():
                    self._adddep(deps, o)
            elif b.last_w is not None:
                self._adddep(deps, b.last_w)
        for b in writes:
            if not b.multi and b.last_w is not None and b.last_w.dom is not dom:
                self._adddep(deps, b.last_w)
            for d, o in b.readers.items():
                if d is not dom:
                    self._adddep(deps, o)
        if is_dma:
            op.signal = True
            if dom.last_op is not None:
                self._adddep(deps, dom.last_op)
        for o in self.barrier_deps[eng]:
            self._adddep(deps, o)
        self.barrier_deps[eng] = []
        for o in deps.values():
            o.signal = True
        op.deps = list(deps.values())
        for b in reads:
            b.readers[dom] = op
        for b in writes:
            if b.multi:
                if b.readers:
                    b.writers = {}
                b.writers[dom] = op
            b.last_w = op
            b.readers = {}
        self.ops[eng].append(op)
        dom.last_op = op
        dom.pending.append(op)
        return op

    def op(self, eng, emit, reads=(), writes=()):
        return self._record(eng, self.cdom[eng], emit, reads, writes, False)

    def dma(self, queue, emit, buf, reads=(), writes=()):
        return self._record(queue, buf.dma, emit, reads, writes, True)

    def flush(self, final=False):
        nc = self.nc
        lasts = []
        for d in self.all_doms:
            if d.pending:
                d.pending[-1].signal = True
                nxt = None
                for o in reversed(d.pending):
                    if o.signal:
                        nxt = o
                    o.next_sig = nxt
                for o in d.pending:
                    if o.signal:
                        d.total += d.step
                        o.sigval = d.total
                d.pending = []
            if d.last_op is not None:
                lasts.append(d.last_op)
        streams = {e: self.ops[e] for e in ENGS}
        self.ops = {e: [] for e in ENGS}
        sched = self

        def run(ename, eng):
            w = sched.waited[ename]
            for o in streams[ename]:
                for dep in o.deps:
                    dep = sched._res(dep)
                    if w.get(dep.dom, 0) < dep.sigval:
                        eng.wait_ge(dep.dom.sem, dep.sigval)
                        w[dep.dom] = dep.sigval
                inst = o.emit(eng)
                sched.n_inst += 1
                if o.signal:
                    inst.then_inc(o.dom.sem, o.dom.step)
                o.emitted = True
            if final and ename == "sp":
                for d in sched.all_doms:
                    if d.total > 0 and w.get(d, 0) < d.total:
                        eng.wait_ge(d.sem, d.total)
                        w[d] = d.total

        with nc.Block() as block:
            @block.tensor
            def _(e):
                run("pe", e)

            @block.scalar
            def _(e):
                run("act", e)

            @block.vector
            def _(e):
                run("dve", e)

            @block.gpsimd
            def _(e):
                run("pool", e)

            @block.sync
            def _(e):
                run("sp", e)
        for e in ENGS:
            self.barrier_deps[e] = list(lasts)


def swpipe(n, stages):
    hi = max(o for o, _ in stages)
    lo = min(o for o, _ in stages)
    for i in range(-hi, n - lo):
        for off, fn in stages:
            k = i + off
            if 0 <= k < n:
                Sched.tag = "%s(%d)" % (fn.__name__, k)
                fn(k)


class T:
    __slots__ = ("t", "b")

    def __init__(self, t, b):
        self.t, self.b = t, b


class TS4:
    __slots__ = ("t", "bs")

    def __init__(self, t, bs):
        self.t, self.bs = t, bs


class Ring:
    def __init__(self, items):
        self.items, self.i = items, 0

    def next(self):
        x = self.items[self.i % len(self.items)]
        self.i += 1
        return x


class Builder:
    def __init__(self, nc, n_dma_sems=96, n_cc_sems=0):
        self.nc = nc
        self.S = Sched(nc, n_dma_sems, n_cc_sems)
        self.dram = {}
        self.dbuf = {}
        self.psum = None
        self.nuniq = 0

    def din(self, name, shape, dt=F32):
        self.dram[name] = self.nc.dram_tensor(name, list(shape), dt, kind="ExternalInput").ap()
        self.dbuf[name] = self.S.buf("d_" + name, multi=True)
        return self.dram[name]

    def dout(self, name, shape, dt=F32):
        self.dram[name] = self.nc.dram_tensor(name, list(shape), dt, kind="ExternalOutput").ap()
        self.dbuf[name] = self.S.buf("d_" + name, multi=True)
        return self.dram[name]

    def dint(self, name, shape, dt=F32):
        self.dram[name] = self.nc.dram_tensor(name, list(shape), dt, kind="Internal").ap()
        self.dbuf[name] = self.S.buf("d_" + name, multi=True)
        return self.dram[name]

    def sb(self, es, name, shape, dt, dma=False):
        self.nuniq += 1
        t = es.enter_context(self.nc.sbuf_tensor("%s_%d" % (name, self.nuniq), list(shape), dt))
        return T(t, self.S.buf(name, dma=dma))

    def alloc_psum(self, es):
        self.psum = []
        for i in range(8):
            t = es.enter_context(self.nc.psum_tensor("ps%d" % i, [128, 512], F32))
            self.psum.append(T(t, self.S.buf("ps%d" % i)))

    def mm(self, out, outb, lhsT, lb, rhs, rb, start, stop):
        self.S.op("pe", lambda e: e.matmul(out, lhsT=lhsT, rhs=rhs, start=start, stop=stop),
                  reads=[lb, rb], writes=[outb])

    def act(self, out, outb, in_, inb, func, scale=1.0, bias=None, extra_reads=()):
        if bias is None:
            self.S.op("act", lambda e: e.activation(out=out, in_=in_, func=func, scale=scale),
                      reads=[inb, *extra_reads], writes=[outb])
        else:
            self.S.op("act", lambda e: e.activation(out=out, in_=in_, func=func, scale=scale, bias=bias),
                      reads=[inb, *extra_reads], writes=[outb])

    def tt(self, eng, out, outb, in0, b0, in1, b1, op):
        self.S.op(eng, lambda e: e.tensor_tensor(out=out, in0=in0, in1=in1, op=op),
                  reads=[b0, b1], writes=[outb])

    def stt(self, eng, out, outb, in0, b0, scalar, sb_, in1, b1, op0, op1):
        rd = [b0, b1] + ([sb_] if sb_ is not None else [])
        self.S.op(eng, lambda e: e.scalar_tensor_tensor(out=out, in0=in0, scalar=scalar, in1=in1,
                                                        op0=op0, op1=op1),
                  reads=rd, writes=[outb])

    def ts(self, eng, out, outb, in0, b0, s1, s2, op0, op1=None, sbufs=()):
        if op1 is None:
            self.S.op(eng, lambda e: e.tensor_scalar(out=out, in0=in0, scalar1=s1, scalar2=None, op0=op0),
                      reads=[b0, *sbufs], writes=[outb])
        else:
            self.S.op(eng, lambda e: e.tensor_scalar(out=out, in0=in0, scalar1=s1, scalar2=s2,
                                                     op0=op0, op1=op1),
                      reads=[b0, *sbufs], writes=[outb])

    def copy(self, eng, out, outb, in_, inb):
        if eng == "act":
            self.act(out, outb, in_, inb, AF.Copy)
        else:
            self.S.op(eng, lambda e: e.tensor_copy(out=out, in_=in_), reads=[inb], writes=[outb])

    def memset(self, eng, out, outb, val):
        self.S.op(eng, lambda e: e.memset(out, val), writes=[outb])

    def load(self, q, dst, dstT, src, srcname):
        self.S.dma(q, lambda e: e.dma_start(out=dst, in_=src), dstT.b,
                   reads=[self.dbuf[srcname]], writes=[dstT.b])

    def store(self, q, dst, dstname, src, srcT):
        self.S.dma(q, lambda e: e.dma_start(out=dst, in_=src), srcT.b,
                   reads=[srcT.b], writes=[self.dbuf[dstname]])

    def tok_alloc(self, es, need_attn=False):
        b = self
        L = {}
        L["hring"] = Ring([b.sb(es, "hsb%d" % i, [128, DC, TS], F32, dma=True) for i in range(2)])
        L["h"] = None
        L["u"] = b.sb(es, "usb", [128, DC, TS], BF16)
        L["ub"] = [b.S.buf("u%d" % i) for i in range(DC)]
        L["a"] = b.sb(es, "asb", [128, FC, TS], BF16)
        L["sq"] = Ring([b.sb(es, "sq%d" % i, [128, TS], BF16) for i in range(2)])
        L["rstd"] = b.sb(es, "rstd", [128, TS], F32)
        L["sg"] = Ring([b.sb(es, "sg%d" % i, [128, TS], F32) for i in range(2)])
        L["wa"] = Ring([b.sb(es, "wa%d" % i, [128, DC, 128], BF16, dma="sw") for i in range(4)])
        L["wd"] = Ring([b.sb(es, "wd%d" % i, [128, FC * 128], BF16, dma="sw") for i in range(2)])
        L["gains"] = b.sb(es, "gains", [128, 7 * DC], F32, dma=True)
        L["onesD"] = b.sb(es, "onesD", [128, 128], BF16)
        L["ev"] = Ring([b.sb(es, "ev%d" % i, [128, TS], BF16, dma=True) for i in range(4)])
        b.memset("pool", L["onesD"].t[:], L["onesD"].b, 1.0 / D)
        b.load("sp", L["gains"].t[:], L["gains"], b.dram["gains"].rearrange("p k c -> p (k c)"), "gains")
        L["pg"] = Ring([b.psum[0], b.psum[1]])
        L["pu"] = Ring([b.psum[2], b.psum[3]])
        L["pd"] = Ring([b.psum[4], b.psum[5]])
        L["pn"] = b.psum[6]
        if need_attn:
            L["attn"] = b.sb(es, "attn_t", [128, NH, TS], BF16, dma=True)
        L["o32"] = Ring([b.sb(es, "o32_%d" % i, [128, TS], F32, dma=True) for i in range(2)])
        return L

    def load_h(self, L, src, j):
        hb = L["hring"].next()
        d = self.dram[src].rearrange("c p t -> p c t")[:, :, j * TS:(j + 1) * TS]
        for c0 in range(0, DC, 4):
            self.load("sp", hb.t[:, c0:c0 + 4, :], hb, d[:, c0:c0 + 4, :], src)
        return hb

    def store_h(self, L, dst, j):
        hb = L["h"]
        d = self.dram[dst].rearrange("c p t -> p c t")[:, :, j * TS:(j + 1) * TS]
        for c0 in range(0, DC, 4):
            self.store("sp", d[:, c0:c0 + 4, :], dst, hb.t[:, c0:c0 + 4, :], hb)

    def rmsnorm(self, L, gidx, out32=None):
        b = self
        h, pn = L["h"], L["pn"]
        for dc in range(DC):
            sq = L["sq"].next()
            b.act(sq.t[:], sq.b, h.t[:, dc, :], h.b, AF.Square)
            b.mm(pn.t[:], pn.b, L["onesD"].t[:], L["onesD"].b, sq.t[:], sq.b, dc == 0, dc == DC - 1)
        r = L["rstd"]
        b.act(r.t[:], r.b, pn.t[:], pn.b, AF.Sqrt, bias=EPS)
        b.S.op("dve", lambda e: e.reciprocal(out=r.t[:], in_=r.t[:]), reads=[r.b], writes=[r.b])
        g = L["gains"]
        for dc in range(DC):
            col = gidx * DC + dc
            if out32 is None:
                dst = L["u"]
                b.stt("dve", dst.t[:, dc, :], L["ub"][dc], h.t[:, dc, :], h.b, g.t[:, col:col + 1], g.b,
                      r.t[:], r.b, ALU.mult, ALU.mult)
            else:
                name, j = out32
                o = L["o32"].next()
                b.stt("dve", o.t[:], o.b, h.t[:, dc, :], h.b, g.t[:, col:col + 1], g.b,
                      r.t[:], r.b, ALU.mult, ALU.mult)
                b.store("sp", b.dram[name][dc][:, j * TS:(j + 1) * TS], name, o.t[:], o)

    def load_wa(self, L, name, idx):
        w = L["wa"].next()
        self.load("pool", w.t[:], w, self.dram[name][idx], name)
        return w

    def ffn(self, L, f):
        b = self
        u, a, h = L["u"], L["a"], L["h"]
        for c in range(FC):
            wg = b.load_wa(L, "wg%d" % f, c)
            wu = b.load_wa(L, "wu%d" % f, c)
            pg, pu = L["pg"].next(), L["pu"].next()
            for dc in range(DC):
                b.mm(pg.t[:], pg.b, wg.t[:, dc, :], wg.b, u.t[:, dc, :], L["ub"][dc], dc == 0, dc == DC - 1)
            for dc in range(DC):
                b.mm(pu.t[:], pu.b, wu.t[:, dc, :], wu.b, u.t[:, dc, :], L["ub"][dc], dc == 0, dc == DC - 1)
            sg = L["sg"].next()
            b.act(sg.t[:], sg.b, pg.t[:], pg.b, AF.Silu)
            b.tt("dve", a.t[:, c, :], a.b, sg.t[:], sg.b, pu.t[:], pu.b, ALU.mult)
        for dmc in range(DC):
            w = L["wd"].next()
            wv = w.t[:, 0:FC * 128].rearrange("p (c m) -> p c m", m=128)
            b.load("pool", wv, w, b.dram["wd%d" % f][dmc], "wd%d" % f)
            pd = L["pd"].next()
            for c in range(FC):
                b.mm(pd.t[:], pd.b, wv[:, c, :], w.b, a.t[:, c, :], a.b, c == 0, c == FC - 1)
            b.stt("dve", h.t[:, dmc, :], h.b, pd.t[:], pd.b, 0.5, None, h.t[:, dmc, :], h.b,
                  ALU.mult, ALU.add)

    def qkv(self, L, l, j, km=None):
        import os
        b = self
        u = L["u"]
        dbg = os.environ.get("QKV_DBG", "")
        if "nokm" in dbg:
            km = None
        pq = Ring([b.psum[0], b.psum[1], b.psum[2], b.psum[3]])
        for un in range(0 if "noqk" not in dbg else 32, 32):
            w = b.load_wa(L, "wqk%d" % l, un)
            p = pq.next()
            for dc in range(DC):
                b.mm(p.t[:], p.b, w.t[:, dc, :], w.b, u.t[:, dc, :], L["ub"][dc], dc == 0, dc == DC - 1)
            ev = L["ev"].next()
            b.copy("dve", ev.t[:], ev.b, p.t[:], p.b)
            if un < 16:
                b.store("sp", b.dram["qT"][un][:, j * TS:(j + 1) * TS], "qT", ev.t[:], ev)
            else:
                nm, ap = b.k_dst(un - 16, j)
                b.store("sp", ap, nm, ev.t[:], ev)
            if km is not None and un >= 24:
                hm = un - 24
                b.S.op("dve", lambda e, hm=hm, p=p: e.reduce_sum(
                    out=km.t[:, hm, 2 * j:2 * j + 2],
                    in_=p.t[:].rearrange("p (b k) -> p b k", b=2), axis=AX.X),
                    reads=[p.b], writes=[km.b])
        pv = Ring([b.psum[4], b.psum[5]])
        for g in range(4):
            ws = []
            for hf in range(2):
                w = L["wd"].next()
                wv = w.t[:, 0:DC * 256].rearrange("p (c n) -> p c n", n=256)
                b.load("pool", wv, w, b.dram["wv%d" % l][2 * g + hf], "wv%d" % l)
                ws.append((w, wv))
            for sub in range(4):
                p = pv.next()
                for hf in range(2):
                    w, wv = ws[hf]
                    for dc in range(DC):
                        b.mm(p.t[:, hf * 256:(hf + 1) * 256], p.b, u.t[:, dc, sub * 128:(sub + 1) * 128],
                             L["ub"][dc], wv[:, dc, :], w.b, dc == 0, dc == DC - 1)
                ev = L["ev"].next()
                b.copy("dve", ev.t[:], ev.b, p.t[:], p.b)
                nm, ap = b.v_dst(j, sub, g)
                b.store("sp", ap, nm, ev.t[:], ev)

    def outproj(self, L, l, j):
        b = self
        h = L["h"]
        attn = L["attn"]
        asrc = b.dram["attnD"].rearrange("h p t -> p h t")[:, :, j * TS:(j + 1) * TS]
        for c0 in range(0, NH, 4):
            b.load("sp", attn.t[:, c0:c0 + 4, :], attn, asrc[:, c0:c0 + 4, :], "attnD")
        for dmc in range(DC):
            w = b.load_wa(L, "wo%d" % l, dmc)
            pd = L["pd"].next()
            for hc in range(NH):
                b.mm(pd.t[:], pd.b, w.t[:, hc, :], w.b, attn.t[:, hc, :], attn.b,
                     hc == 0, hc == NH - 1)
            b.tt("dve", h.t[:, dmc, :], h.b, pd.t[:], pd.b, h.t[:, dmc, :], h.b, ALU.add)

    def attn_alloc(self, es):
        b = self
        A = {}
        def sb4(name, shape):
            t = b.sb(es, name, shape, BF16)
            return TS4(t.t, [b.S.buf("%s_%d" % (name, a), dma=True) for a in range(4)])
        A["kT"] = Ring([sb4("kT%d" % i, [128, SEQ]) for i in range(2)])
        A["v"] = Ring([sb4("v%d" % i, [128, 32, HD]) for i in range(2)])
        A["q"] = Ring([b.sb(es, "q%d" % i, [128, NTOK], BF16, dma=True) for i in range(2)])
        A["e"] = Ring([b.sb(es, "e%d" % i, [128, TS], F32) for i in range(6)])
        A["p"] = Ring([b.sb(es, "p%d" % i, [128, TS], BF16) for i in range(4)])
        A["ao"] = Ring([b.sb(es, "ao%d" % i, [128, TS], BF16, dma=True) for i in range(2)])
        A["cst"] = b.sb(es, "cst", [128, CSTW], BF16, dma="sw")
        b.load("pool", A["cst"].t[:], A["cst"], b.dram["cstb"], "cstb")
        return A

    def load_head(self, A, h):
        b = self
        kT, v, q = A["kT"].next(), A["v"].next(), A["q"].next()
        def ld(dst, buf, src, nm):
            b.S.dma("sp", lambda e: e.dma_start(out=dst, in_=src), buf, reads=[b.dbuf[nm]], writes=[buf])

        if b.fused:
            for a in range(4):
                for r in range(2):
                    Tt = 2 * a + r
                    nm = "kTg%d" % a
                    ld(kT.t[:, Tt * TS:(Tt + 1) * TS], kT.bs[a],
                       b.dram[nm][r * 2048 + h * HD:r * 2048 + (h + 1) * HD, :], nm)
            for a in range(4):
                for r in range(2):
                    Tt = 2 * a + r
                    nm = "vg%d" % a
                    srcv = b.dram[nm][r * TS:(r + 1) * TS, h * HD:(h + 1) * HD].rearrange("(q p) d -> p q d", p=128)
                    ld(v.t[:, Tt * 4:(Tt + 1) * 4, :], v.bs[a], srcv, nm)
        else:
            for r in range(2):
                for a in range(4):
                    Tt = 2 * a + r
                    src = b.dram["kTg"][r][h][:, a * TS:(a + 1) * TS]
                    ld(kT.t[:, Tt * TS:(Tt + 1) * TS], kT.bs[a], src, "kTg")
                    srcv = b.dram["vg"][r][a * TS:(a + 1) * TS, h * HD:(h + 1) * HD].rearrange(
                        "(q p) d -> p q d", p=128)
                    ld(v.t[:, Tt * 4:(Tt + 1) * 4, :], v.bs[a], srcv, "vg")
        b.load("sp", q.t[:], q, b.dram["qT"][h], "qT")
        return kT, v, q

    fused = False

    def k_dst(self, h, j):
        if self.fused:
            nm = "kTl%d" % j
            return nm, self.dram[nm][h * HD:(h + 1) * HD, :]
        return "kTl", self.dram["kTl"][h][:, j * TS:(j + 1) * TS]

    def v_dst(self, j, sub, g):
        if self.fused:
            nm = "vl%d" % j
            return nm, self.dram[nm][sub * 128:(sub + 1) * 128, g * 512:(g + 1) * 512]
        r0 = j * TS + sub * 128
        return "vl", self.dram["vl"][r0:r0 + 128, g * 512:(g + 1) * 512]

    def km_src(self, r, hm):
        if self.fused:
            return self.dram["kmg"][r * 1024 + hm * 128:r * 1024 + (hm + 1) * 128, :].rearrange("p (a c) -> p a c", c=2)
        return self.dram["kmg"][r][hm].rearrange("p (a c) -> p a c", c=2)

    def allgather(self, src, dst):
        S = self.S
        d = S.cc_doms[S.cc_next % len(S.cc_doms)]
        S.cc_next += 1
        cb = Buf("cc", d)
        si, so = self.dram[src].opt(), self.dram[dst].opt()
        S.dma("pool", lambda e: e.collective_compute("AllGather", ALU.bypass,
                                                     replica_groups=[[0, 1], [2, 3], [4, 5], [6, 7]],
                                                     ins=[si], outs=[so]),
              cb, reads=[self.dbuf[src]], writes=[self.dbuf[dst]])

    def exchange_tile(self, j):
        self.allgather("kTl%d" % j, "kTg%d" % j)
        self.allgather("vl%d" % j, "vg%d" % j)

    def attn_out(self, A, src, srcb, h, j, rden=None):
        b = self
        ao = A["ao"].next()
        if rden is None:
            b.copy("dve", ao.t[:], ao.b, src, srcb)
        else:
            b.tt("dve", ao.t[:], ao.b, src, srcb, rden.t[:], rden.b, ALU.mult)
        b.store("sp", b.dram["attnD"][h][:, j * TS:(j + 1) * TS], "attnD", ao.t[:], ao)

    def attn0(self, es):
        b = self
        A = b.attn_alloc(es)
        ones = A["cst"].t[:, 0:128]
        cb = A["cst"].b
        selrow = A["cst"].t[0:16, 256:256 + 16 * 128].rearrange("p (n m) -> p n m", m=128)
        rel = b.sb(es, "rel", [32, 16], F32, dma=True)
        erel = b.sb(es, "erel", [32, 16], F32)
        Mt = b.sb(es, "Mt", [32, 2, XL], F32, dma=True)
        fl = b.sb(es, "fl", [8, 2, XL], F32, dma=True)
        tabfar = b.sb(es, "tabfar", [128, 8], F32, dma=True)
        Jm = b.sb(es, "Jm", [128, 128], F32)
        b.load("sp", rel.t[:], rel, b.dram["rel_bias"], "rel_bias")
        b.load("sp", Mt.t[:, 0, :], Mt, b.dram["Mdil"], "Mdil")
        b.load("sp", Mt.t[:, 1, :], Mt, b.dram["Mmoba"], "Mmoba")
        b.load("sp", tabfar.t[:], tabfar, b.dram["rel_bias"][31:32, 8:16].partition_broadcast(128), "rel_bias")
        b.act(erel.t[:], erel.b, rel.t[:], rel.b, AF.Exp)
        b.memset("pool", Jm.t[:], Jm.b, 0.0)
        b.S.op("pool", lambda e: e.affine_select(out=Jm.t[:], in_=Jm.t[:], pattern=[[1, 128]],
                                                 compare_op=ALU.not_equal, fill=1.0, base=-127,
                                                 channel_multiplier=1),
               reads=[Jm.b], writes=[Jm.b])
        pm = b.psum[6]
        for k in range(2):
            for c in range(XL // 512):
                b.mm(pm.t[0:8, :], pm.b, erel.t[:, 8 * k:8 * k + 8], erel.b,
                     Mt.t[:, k, c * 512:(c + 1) * 512], Mt.b, True, True)
                b.copy("dve", fl.t[:, k, c * 512:(c + 1) * 512], fl.b, pm.t[0:8, :], pm.b)
            b.store("sp", b.dram["flat"][8 * k:8 * k + 8, :], "flat", fl.t[:, k, :], fl)
        pastneg = b.sb(es, "pastneg", [128, 256], F32, dma=True)
        notown = b.sb(es, "notown", [128, 256], F32, dma=True)
        b.load("sp", pastneg.t[:], pastneg, b.dram["pastneg"], "pastneg")
        b.load("sp", notown.t[:], notown, b.dram["notown"], "notown")
        ident = b.sb(es, "ident", [128, 128], F32)
        b.memset("pool", ident.t[:], ident.b, 0.0)
        b.S.op("pool", lambda e: e.affine_select(out=ident.t[:], in_=ident.t[:], pattern=[[-1, 128]],
                                                 compare_op=ALU.not_equal, fill=1.0, base=0,
                                                 channel_multiplier=1),
               reads=[ident.b], writes=[ident.b])
        gsp = b.sb(es, "gsp", [128, GW], F32, dma=True)
        gs_ring = Ring([b.sb(es, "gs%d" % i, [128, GW], F32) for i in range(2)])
        kmf = b.sb(es, "kmf", [128, 16], F32, dma=True)
        kmb = b.sb(es, "kmb", [128, 16], BF16)
        gm = b.sb(es, "gm", [128, 64], F32)
        mx8 = b.sb(es, "mx8", [128, 4, 8], F32)
        thr = b.sb(es, "thr", [128, 4], F32)
        nsel = b.sb(es, "nsel", [128, 64], F32)
        nselT = Ring([b.sb(es, "nselT%d" % i, [16, TS], BF16) for i in range(2)])
        rden = b.sb(es, "rden", [128, TS], F32)
        lden = b.sb(es, "lden", [128, TS], F32)
        pfar = Ring([b.sb(es, "pfar%d" % i, [128, TS], BF16) for i in range(6)])
        ghi = b.sb(es, "ghi", [128, GW], BF16)
        glo = b.sb(es, "glo", [128, GW], BF16)
        Jb = b.sb(es, "Jb", [128, 128], BF16)
        b.copy("pool", Jb.t[:], Jb.b, Jm.t[:], Jm.b)
        pj_ring = Ring([b.psum[6], b.psum[7]])
        ps_ring = Ring([b.psum[0], b.psum[1]])
        po_ring = Ring([b.psum[2], b.psum[3]])
        pden_ring = Ring([b.psum[4], b.psum[5]])
        mult_eng = Ring(["dve"])

        def prep_load(h):
            hd = b.load_head(A, h)
            src = bass.AP(b.dram["flat"].tensor, h * XL, [[1, 128], [1, GW]])
            b.load("sp", gsp.t[:], gsp, src, "flat")
            b.copy("pool", ghi.t[:], ghi.b, gsp.t[:], gsp.b)
            b.tt("pool", glo.t[:], glo.b, gsp.t[:], gsp.b, ghi.t[:], ghi.b, ALU.subtract)
            return hd

        def prep_table(hd):
            gs = gs_ring.next()
            for c in range((GW + 511) // 512):
                w = min(512, GW - c * 512)
                pj = pj_ring.next()
                b.mm(pj.t[:, 0:w], pj.b, Jb.t[:], Jb.b, ghi.t[:, c * 512:c * 512 + w], ghi.b, True, False)
                b.mm(pj.t[:, 0:w], pj.b, Jb.t[:], Jb.b, glo.t[:, c * 512:c * 512 + w], glo.b, False, True)
                b.copy("dve", gs.t[:, c * 512:c * 512 + w], gs.b, pj.t[:, 0:w], pj.b)
            return hd, gs

        def load_km(h):
            hm = h - 8
            for r in range(2):
                dst = kmf.t[:].rearrange("p (a r c) -> p a r c", r=2, c=2)[:, :, r, :]
                b.load("sp", dst, kmf, b.km_src(r, hm), "kmg")
            b.copy("dve", kmb.t[:], kmb.b, kmf.t[:], kmf.b)

        def gate(h, j, q):
            pg = b.psum[7]
            for sub in range(4):
                b.mm(pg.t[:, sub * 16:(sub + 1) * 16], pg.b,
                     q.t[:, j * TS + sub * 128:j * TS + (sub + 1) * 128], q.b, kmb.t[:], kmb.b, True, True)
            b.tt("dve", gm.t[:], gm.b, pg.t[:, 0:64], pg.b, pastneg.t[:, j * 64:(j + 1) * 64], pastneg.b, ALU.add)
            for sub in range(4):
                b.S.op("dve", lambda e, sub=sub: e.max(out=mx8.t[:, sub, :], in_=gm.t[:, sub * 16:(sub + 1) * 16]),
                       reads=[gm.b], writes=[mx8.b])
            b.ts("dve", thr.t[:], thr.b, mx8.t[:, :, 2], mx8.b, -1e29, None, ALU.max)
            for sub in range(4):
                b.ts("dve", nsel.t[:, sub * 16:(sub + 1) * 16], nsel.b, gm.t[:, sub * 16:(sub + 1) * 16], gm.b,
                     thr.t[:, sub:sub + 1], None, ALU.is_lt, sbufs=[thr.b])
            b.tt("dve", nsel.t[:], nsel.b, nsel.t[:], nsel.b, notown.t[:, j * 64:(j + 1) * 64], notown.b, ALU.mult)
            for sub in range(4):
                b.S.op("pe", lambda e, sub=sub: e.transpose(b.psum[6].t[0:16, sub * 128:(sub + 1) * 128],
                                                            nsel.t[:, sub * 16:(sub + 1) * 16], ident.t[:]),
                       reads=[nsel.b, ident.b], writes=[b.psum[6].b])
            nT = nselT.next()
            b.copy("dve", nT.t[:], nT.b, b.psum[6].t[0:16, :], b.psum[6].b)
            return nT

        steps = []
        for h in range(NH):
            for j in range(NT):
                kb_lo = 0 if h >= 8 else max(0, 8 * j - 16)
                kbs = list(range(kb_lo, 8 * j + 8))
                for k, kb in enumerate(kbs):
                    steps.append((h, j, k, kb, len(kbs)))
        G = len(steps)
        st = [dict() for _ in range(G)]
        heads = {0: prep_table(prep_load(0))}
        pend = {}
        tiles = {}

        def fS(g):
            h, j, k, kb, n = steps[g]
            moba = h >= 8
            if j == 0 and k == 6 and h + 1 < NH:
                pend[h + 1] = prep_load(h + 1)
            if j == 2 and k == 0 and h + 1 < NH:
                heads[h + 1] = prep_table(pend.pop(h + 1))
            (kT, v, q), gs = heads[h]
            if k == 0:
                tiles.setdefault((h, j), {})
                if g == 0 and moba:
                    load_km(h)
                    tiles[(h, j)]["nT"] = gate(h, j, q)
                h2, j2 = (h, j + 1) if j + 1 < NT else (h + 1, 0)
                if h2 < NH and h2 >= 8:
                    if j2 == 0:
                        load_km(h2)
                    tiles.setdefault((h2, j2), {})["nT"] = gate(h2, j2, heads[h2][0][2])
            ps = ps_ring.next()
            b.mm(ps.t[:], ps.b, kT.t[:, kb * 128:(kb + 1) * 128], kT.bs[kb // 8], q.t[:, j * TS:(j + 1) * TS], q.b,
                 True, not moba)
            if moba:
                nT = tiles[(h, j)]["nT"]
                b.mm(ps.t[:], ps.b, selrow[:, kb // 2, :], cb, nT.t[:], nT.b, False, True)
            st[g]["ps"] = ps

        def fE(g):
            h, j, k, kb, n = steps[g]
            moba = h >= 8
            ps = st[g].pop("ps")
            delta = (8 * j - kb) * 128
            far = moba and delta >= 1664
            st[g]["delta"] = delta
            if far:
                p = pfar.next()
                b.act(p.t[:], p.b, ps.t[:], ps.b, AF.Exp, scale=SCALE,
                      bias=tabfar.t[:, h - 8:h - 7], extra_reads=[tabfar.b])
                st[g]["p"] = p
            else:
                e_ = A["e"].next()
                b.act(e_.t[:], e_.b, ps.t[:], ps.b, AF.Exp, scale=SCALE)
                st[g]["e"] = e_

        def fP(g):
            h = steps[g][0]
            if "e" in st[g]:
                gs = heads[h][1]
                e_ = st[g].pop("e")
                p = A["p"].next()
                i0 = st[g]["delta"] + 896
                b.tt(mult_eng.next(), p.t[:], p.b, e_.t[:], e_.b, gs.t[:, i0:i0 + TS], gs.b, ALU.mult)
                st[g]["p"] = p

        def fPV(g):
            h, j, k, kb, n = steps[g]
            tl = tiles[(h, j)]
            if k == 0:
                tl["po"], tl["pden"] = po_ring.next(), pden_ring.next()
            po, pden = tl["po"], tl["pden"]
            v = heads[h][0][1]
            p = st[g].pop("p")
            b.mm(po.t[:], po.b, v.t[:, kb, :], v.bs[kb // 8], p.t[:], p.b, k == 0, k == n - 1)
            b.mm(pden.t[:], pden.b, ones, cb, p.t[:], p.b, k == 0, k == n - 1)
            if k == n - 1:
                b.act(lden.t[:], lden.b, pden.t[:], pden.b, AF.Ln)
                b.act(rden.t[:], rden.b, lden.t[:], lden.b, AF.Exp, scale=-1.0)
                b.attn_out(A, po.t[:], po.b, h, j, rden=rden)

        swpipe(G, [(3, fS), (3, fE), (1, fP), (0, fPV)])

    def attn1(self, es):
        b = self
        A = b.attn_alloc(es)
        ones = A["cst"].t[:, 0:128]
        tri = A["cst"].t[:, 128:256]
        cb = A["cst"].b
        tri2 = A["cst"].t[:, 256 + 2048:256 + 2048 + 128]
        sbm = b.sb(es, "sbm", [128, 8, TS], BF16, dma="sw")
        b.load("pool", sbm.t[:], sbm, b.dram["sbmask"], "sbmask")
        Sb_ring = Ring([b.sb(es, "Sb%d" % i, [128, TS], BF16) for i in range(5)])
        sp_ring = Ring([b.sb(es, "sp%d" % i, [128, TS], BF16) for i in range(4)])
        ew_ring = Ring([b.sb(es, "ew%d" % i, [128, TS], F32) for i in range(2)])
        pz_ring = Ring([b.psum[0], b.psum[1]])
        pw_ring = Ring([b.psum[2], b.psum[3], b.psum[4]])
        po_ring = Ring([b.psum[5], b.psum[6]])
        steps = []
        for h in range(NH):
            for j in range(NT):
                kbs = list(range(8 * j + 7, -1, -1))
                for k, kb in enumerate(kbs):
                    steps.append((h, j, k, kb, len(kbs)))
        G = len(steps)
        st = [dict() for _ in range(G)]
        heads = {0: b.load_head(A, 0)}
        pos = {}

        def fZ(g):
            h, j, k, kb, n = steps[g]
            if j == 0 and k == 6 and h + 1 < NH:
                heads[h + 1] = b.load_head(A, h + 1)
            kT, v, q = heads[h]
            pz = pz_ring.next()
            diag = kb >= 8 * j
            b.mm(pz.t[:], pz.b, kT.t[:, kb * 128:(kb + 1) * 128], kT.bs[kb // 8], q.t[:, j * TS:(j + 1) * TS], q.b,
                 True, not diag)
            if diag:
                b.mm(pz.t[:], pz.b, tri2, cb, sbm.t[:, kb - 8 * j, :], sbm.b, False, True)
            st[g]["pz"] = pz

        def fE(g):
            pz = st[g].pop("pz")
            e_ = A["e"].next()
            b.act(e_.t[:], e_.b, pz.t[:], pz.b, AF.Exp, scale=SCALE)
            st[g]["e"] = e_

        def fSP(g):
            e_ = st[g]["e"]
            sp = sp_ring.next()
            b.act(sp.t[:], sp.b, e_.t[:], e_.b, AF.Ln, bias=1.0)
            st[g]["sp"] = sp

        def fPW(g):
            k = steps[g][2]
            sp = st[g]["sp"]
            pw = pw_ring.next()
            b.mm(pw.t[:], pw.b, tri, cb, sp.t[:], sp.b, True, k == 0)
            if k > 0:
                Sp = st[g - 1]["Sb"]
                b.mm(pw.t[:], pw.b, ones, cb, Sp.t[:], Sp.b, False, True)
            st[g]["pw"] = pw

        def fS(g):
            k = steps[g][2]
            sp = st[g]["sp"]
            Sn = Sb_ring.next()
            if k == 0:
                b.copy("dve", Sn.t[:], Sn.b, sp.t[:], sp.b)
            else:
                Sp = st[g - 1]["Sb"]
                b.tt("dve", Sn.t[:], Sn.b, Sp.t[:], Sp.b, sp.t[:], sp.b, ALU.add)
            st[g]["Sb"] = Sn

        def fEW(g):
            pw = st[g].pop("pw")
            ew = ew_ring.next()
            b.act(ew.t[:], ew.b, pw.t[:], pw.b, AF.Exp, scale=-1.0)
            st[g]["ew"] = ew

        def fA(g):
            e_, ew = st[g].pop("e"), st[g].pop("ew")
            a_ = A["p"].next()
            b.tt("dve", a_.t[:], a_.b, e_.t[:], e_.b, ew.t[:], ew.b, ALU.mult)
            st[g]["a"] = a_

        def fPV(g):
            h, j, k, kb, n = steps[g]
            a_ = st[g].pop("a")
            if k == 0:
                pos[(h, j)] = po_ring.next()
            po = pos[(h, j)]
            v = heads[h][1]
            b.mm(po.t[:], po.b, v.t[:, kb, :], v.bs[kb // 8], a_.t[:], a_.b, k == 0, k == n - 1)
            if k == n - 1:
                b.attn_out(A, po.t[:], po.b, h, j)
                if g >= 8:
                    st[g - 8].clear()

        swpipe(G, [(3, fZ), (3, fE), (0, fEW), (0, fA), (3, fSP), (1, fPW), (3, fS), (-2, fPV)])


def decl_ffn_w(b, f):
    b.din("wg%d" % f, [FC, 128, DC, 128])
    b.din("wu%d" % f, [FC, 128, DC, 128])
    b.din("wd%d" % f, [DC, 128, FC, 128])


def decl_qkv_w(b, l):
    b.din("wqk%d" % l, [32, 128, DC, 128])
    b.din("wv%d" % l, [8, 128, DC, 256])


def build_L1(nt=NT, level=9):
    nc = bass.Bass("TRN2", target_bir_lowering=False)
    b = Builder(nc)
    b.din("xT", [DC, 128, NTOK])
    b.din("gains", [128, 7, DC])
    decl_ffn_w(b, 0)
    decl_qkv_w(b, 0)
    b.dout("hT", [DC, 128, NTOK])
    b.dout("qT", [NH, 128, NTOK], BF16)
    b.dout("kTl", [NH, 128, NTOK], BF16)
    b.dout("vl", [NTOK, D], BF16)
    b.dout("kml", [8, 128, 8])
    with ExitStack() as es:
        b.alloc_psum(es)
        L = b.tok_alloc(es)
        km = b.sb(es, "km", [128, 8, 8], F32, dma=True)
        for j in range(nt):
            b.load_h(L, "xT", j)
            if level >= 1:
                b.rmsnorm(L, 0)
            if level >= 2:
                b.ffn(L, 0)
            if level >= 3:
                b.rmsnorm(L, 1)
                b.qkv(L, 0, j, km=km)
            b.store_h(L, "hT", j)
        if level >= 3:
            b.ts("dve", km.t[:], km.b, km.t[:], km.b, 1.0 / 256.0, None, ALU.mult)
            b.store("sp", b.dram["kml"].rearrange("h p n -> p h n"), "kml", km.t[:], km)
        b.S.flush(final=True)
    return nc, b


def decl_attn_in(b, layer0, attn_only=False):
    if not attn_only:
        b.din("hTin", [DC, 128, NTOK])
    b.din("qT", [NH, 128, NTOK], BF16)
    b.din("kTg", [2, NH, 128, NTOK], BF16)
    b.din("vg", [2, NTOK, D], BF16)
    b.din("cstb", [128, CSTW])
    if layer0:
        b.din("kmg", [2, 8, 128, 8])
        b.din("rel_bias", [32, 16])
        b.din("Mdil", [32, XL])
        b.din("Mmoba", [32, XL])
        b.din("pastneg", [128, 256])
        b.din("notown", [128, 256])
        b.dint("flat", [16, XL])
    else:
        b.din("sbmask", [128, 8, TS])


def build_L2(nt=NT, attn_only=False):
    nc = bass.Bass("TRN2", target_bir_lowering=False)
    b = Builder(nc)
    decl_attn_in(b, True, attn_only)
    if not attn_only:
        b.din("gains", [128, 7, DC])
        b.din("wo0", [DC, 128, NH, 128])
        decl_ffn_w(b, 1)
        decl_ffn_w(b, 2)
        decl_qkv_w(b, 1)
        b.dout("hT", [DC, 128, NTOK])
        b.dout("qTo", [NH, 128, NTOK], BF16)
        b.dout("kTl", [NH, 128, NTOK], BF16)
        b.dout("vl", [NTOK, D], BF16)
    if attn_only:
        b.dout("attn_dbg", [NH, 128, NTOK], BF16)
    if attn_only:
        b.dram["attnD"], b.dbuf["attnD"] = b.dram["attn_dbg"], b.dbuf["attn_dbg"]
    else:
        b.dint("attnD", [NH, 128, NTOK], BF16)
    with ExitStack() as es0:
        b.alloc_psum(es0)
        with ExitStack() as es:
            b.attn0(es)
            b.S.flush(final=attn_only)
        with ExitStack() as es:
            if not attn_only:
                L = b.tok_alloc(es, need_attn=True)
                b.dram["qT_in"] = b.dram["qT"]
                b.dram["qT"] = b.dram["qTo"]
                b.dbuf["qT"] = b.dbuf["qTo"]
                for j in range(nt):
                    b.load_h(L, "hTin", j)
                    b.outproj(L, 0, j)
                    b.rmsnorm(L, 2)
                    b.ffn(L, 1)
                    b.rmsnorm(L, 3)
                    b.ffn(L, 2)
                    b.rmsnorm(L, 4)
                    b.qkv(L, 1, j)
                    b.store_h(L, "hT", j)
                b.S.flush(final=True)
    return nc, b


def build_L3(nt=NT, attn_only=False):
    nc = bass.Bass("TRN2", target_bir_lowering=False)
    b = Builder(nc)
    decl_attn_in(b, False, attn_only)
    if not attn_only:
        b.din("gains", [128, 7, DC])
        b.din("wo1", [DC, 128, NH, 128])
        decl_ffn_w(b, 3)
        b.dout("outT", [DC, 128, NTOK])
    if attn_only:
        b.dout("attn_dbg", [NH, 128, NTOK], BF16)
    if attn_only:
        b.dram["attnD"], b.dbuf["attnD"] = b.dram["attn_dbg"], b.dbuf["attn_dbg"]
    else:
        b.dint("attnD", [NH, 128, NTOK], BF16)
    with ExitStack() as es0:
        b.alloc_psum(es0)
        with ExitStack() as es:
            b.attn1(es)
            b.S.flush(final=attn_only)
        with ExitStack() as es:
            if not attn_only:
                L = b.tok_alloc(es, need_attn=True)
                for j in range(nt):
                    b.load_h(L, "hTin", j)
                    b.outproj(L, 1, j)
                    b.rmsnorm(L, 5)
                    b.ffn(L, 3)
                    b.rmsnorm(L, 6, out32=("outT", j))
                b.S.flush(final=True)
    return nc, b


def build_fused():
    nc = bass.Bass("TRN2", target_bir_lowering=False)
    b = Builder(nc, n_dma_sems=30, n_cc_sems=2)
    b.fused = True
    b.din("xT", [DC, 128, NTOK])
    b.din("gains", [128, 7, DC])
    for f in range(4):
        decl_ffn_w(b, f)
    for l in range(2):
        decl_qkv_w(b, l)
        b.din("wo%d" % l, [DC, 128, NH, 128])
    b.din("cstb", [128, CSTW])
    b.din("rel_bias", [32, 16])
    b.din("Mdil", [32, XL])
    b.din("Mmoba", [32, XL])
    b.din("pastneg", [128, 256])
    b.din("notown", [128, 256])
    b.din("sbmask", [128, 8, TS])
    b.dout("outT", [DC, 128, NTOK])
    b.dint("hT", [DC, 128, NTOK])
    b.dint("qT", [NH, 128, NTOK], BF16)
    for j in range(NT):
        b.dint("kTl%d" % j, [NH * HD, TS], BF16)
        b.dint("kTg%d" % j, [2 * NH * HD, TS], BF16)
        b.dint("vl%d" % j, [TS, D], BF16)
        b.dint("vg%d" % j, [2 * TS, D], BF16)
    b.dint("kml", [8 * 128, 8])
    b.dint("kmg", [2 * 8 * 128, 8])
    b.dint("flat", [16, XL])
    b.dint("attnD", [NH, 128, NTOK], BF16)
    with ExitStack() as es0:
        b.alloc_psum(es0)
        with ExitStack() as es:
            L = b.tok_alloc(es)
            km = b.sb(es, "km", [128, 8, 8], F32, dma=True)
            nxt = b.load_h(L, "xT", 0)
            for j in range(NT):
                L["h"] = nxt
                if j + 1 < NT:
                    nxt = b.load_h(L, "xT", j + 1)
                b.rmsnorm(L, 0)
                b.ffn(L, 0)
                b.rmsnorm(L, 1)
                b.qkv(L, 0, j, km=km)
                b.store_h(L, "hT", j)
                b.exchange_tile(j)
            b.ts("dve", km.t[:], km.b, km.t[:], km.b, 1.0 / 256.0, None, ALU.mult)
            b.store("sp", b.dram["kml"].rearrange("(h p) n -> p h n", p=128), "kml", km.t[:], km)
            b.allgather("kml", "kmg")
            b.S.flush()
        with ExitStack() as es:
            b.attn0(es)
            b.S.flush()
        with ExitStack() as es:
            L = b.tok_alloc(es, need_attn=True)
            nxt = b.load_h(L, "hT", 0)
            for j in range(NT):
                L["h"] = nxt
                if j + 1 < NT:
                    nxt = b.load_h(L, "hT", j + 1)
                b.outproj(L, 0, j)
                b.rmsnorm(L, 2)
                b.ffn(L, 1)
                b.rmsnorm(L, 3)
                b.ffn(L, 2)
                b.rmsnorm(L, 4)
                b.qkv(L, 1, j)
                b.store_h(L, "hT", j)
                b.exchange_tile(j)
            b.S.flush()
        with ExitStack() as es:
            b.attn1(es)
            b.S.flush()
        with ExitStack() as es:
            L = b.tok_alloc(es, need_attn=True)
            nxt = b.load_h(L, "hT", 0)
            for j in range(NT):
                L["h"] = nxt
                if j + 1 < NT:
                    nxt = b.load_h(L, "hT", j + 1)
                b.outproj(L, 1, j)
                b.rmsnorm(L, 5)
                b.ffn(L, 3)
                b.rmsnorm(L, 6, out32=("outT", j))
            b.S.flush(final=True)
    return nc, b


def rel_bucket_np(d):
    d = np.maximum(d, 0)
    df = np.maximum(d, 1).astype(np.float32)
    large = 16 + (np.log(df / np.float32(16)) / np.float32(math.log(2048 / 16)) * np.float32(16)).astype(np.int32)
    large = np.minimum(large, 31)
    return np.where(d < 16, d, large)


def pos_tables(r):
    x = np.arange(XL)
    d = x - 1023 + r * 512
    bk = rel_bucket_np(d)
    mult = ((d <= 128).astype(np.float32) + ((d % 4 == 0) & (d <= 512)) + ((d % 16 == 0) & (d <= 2048)))
    valid = d >= 0
    Mdil = np.zeros((32, XL), np.float32)
    Mmoba = np.zeros((32, XL), np.float32)
    Mdil[bk[valid], x[valid]] = mult[valid]
    Mmoba[bk[valid], x[valid]] = 1.0
    pastneg = np.zeros((128, 4, 4, 16), np.float32)
    notown = np.ones((128, 4, 4, 16), np.float32)
    p = np.arange(128)
    for j in range(4):
        for sub in range(4):
            pos = (2 * j + r) * 512 + sub * 128 + p
            own = pos // 256
            n = np.arange(16)[None, :]
            pastneg[:, j, sub, :] = np.where(n < own[:, None], 0.0, -1e30)
            notown[:, j, sub, :] = np.where(n == own[:, None], 0.0, 1.0)
    s = np.arange(128)[:, None]
    t = np.arange(512)[None, :]
    sbmask = np.zeros((128, 8, 512), np.float32)
    tt_ = np.arange(512)
    for m in range(8):
        c = r * 512 - m * 128
        ks = tt_ + c
        ok = ks <= 127
        sbmask[np.maximum(ks[ok], 0), m, tt_[ok]] = -BIG
    return dict(Mdil=Mdil, Mmoba=Mmoba, pastneg=pastneg.reshape(128, 256),
                notown=notown.reshape(128, 256), sbmask=sbmask)


def const_tables():
    cst = np.zeros((128, CSTW), np.float32)
    cst[:, 0:128] = 1.0
    k = np.arange(128)[:, None]
    m = np.arange(128)[None, :]
    cst[:, 128:256] = (k >= m)
    sel = np.zeros((16, 16, 128), np.float32)
    for n in range(16):
        sel[n, n, :] = -BIG
    cst[0:16, 256:256 + 2048] = sel.reshape(16, 16 * 128)
    cst[:, 256 + 2048:] = (k <= m)
    return cst


def lay_unit(W, ncol):
    K, N = W.shape
    return np.ascontiguousarray(W.reshape(K // 128, 128, N // ncol, ncol).transpose(2, 1, 0, 3))


def prep_weights(inp):
    w = {}
    for i in range(2):
        for k in range(2):
            f = 2 * i + k
            w["wg%d" % f] = lay_unit(inp["ffn_w_gate"][i, k], 128)
            w["wu%d" % f] = lay_unit(inp["ffn_w_up"][i, k], 128)
            w["wd%d" % f] = lay_unit(inp["ffn_w_down"][i, k], 128)
    for l, (wq, wo) in enumerate([(inp["w_qkv_even"][0], inp["w_out_even"][0]),
                                  (inp["w_qkv_odd"][0], inp["w_out_odd"][0])]):
        w["wqk%d" % l] = lay_unit(wq[:, :2 * D], 128)
        w["wv%d" % l] = lay_unit(wq[:, 2 * D:], 256)
        w["wo%d" % l] = lay_unit(wo, 128)
    g = np.concatenate([inp["ln_gains"].reshape(6, D), inp["final_gain"][None]], 0)
    w["gains"] = np.ascontiguousarray(g.reshape(7, DC, 128).transpose(2, 0, 1))
    return w


def tok_index(r):
    return np.concatenate([np.arange((2 * j + r) * 512, (2 * j + r + 1) * 512) for j in range(4)])


_CACHE = {}


def _prog(name, fn):
    if name not in _CACHE:
        _CACHE[name] = fn()[0]
    return _CACHE[name]


def kernel(x, ln_gains, ffn_w_gate, ffn_w_up, ffn_w_down, w_qkv_even, w_out_even,
           w_qkv_odd, w_out_odd, rel_bias, final_gain):
    inp = dict(x=x, ln_gains=ln_gains, ffn_w_gate=ffn_w_gate, ffn_w_up=ffn_w_up,
               ffn_w_down=ffn_w_down, w_qkv_even=w_qkv_even, w_out_even=w_out_even,
               w_qkv_odd=w_qkv_odd, w_out_odd=w_out_odd, rel_bias=rel_bias, final_gain=final_gain)
    inp = {k: np.asarray(v, np.float32) for k, v in inp.items()}
    W = prep_weights(inp)
    cores = list(range(8))
    cst = const_tables()
    ptab = [pos_tables(r) for r in range(2)]

    def pick(names):
        return {n: W[n] for n in names}

    maps = []
    for c in cores:
        bb, r = c // 2, c % 2
        xs = inp["x"][bb][tok_index(r)]
        m = dict(W)
        m.update(xT=np.ascontiguousarray(xs.T.reshape(DC, 128, NTOK)), cstb=cst, rel_bias=inp["rel_bias"],
                 Mdil=ptab[r]["Mdil"], Mmoba=ptab[r]["Mmoba"], pastneg=ptab[r]["pastneg"],
                 notown=ptab[r]["notown"], sbmask=ptab[r]["sbmask"])
        maps.append(m)
    res = run_bass_kernel_spmd(_prog("fused", build_fused), maps, core_ids=cores).results
    out = np.empty((4, SEQ, D), np.float32)
    for c in cores:
        bb, r = c // 2, c % 2
        oT = np.asarray(res[c]["outT"]).reshape(D, NTOK)
        out[bb][tok_index(r)] = oT.T
    return out
```

```python
import math
from contextlib import ExitStack

import numpy as np
import ml_dtypes
import concourse.bass as bass
import concourse.mybir as mybir
from concourse.bass_utils import run_bass_kernel_spmd

F32 = mybir.dt.float32
BF16 = mybir.dt.bfloat16
ALU = mybir.AluOpType
AF = mybir.ActivationFunctionType
AX = mybir.AxisListType

D = 2048
DC = 16
DFF = 5632
FC = 44
SEQ = 4096
NTOK = 2048
NT = 4
TS = 512
HD = 128
NH = 16
SCALE = HD ** -0.5
EPS = 1e-6
XL = 3584
GW = 3456
BIG = 30000.0
CSTW = 256 + 16 * 128 + 128


class Dom:
    def __init__(self, name, sem, step):
        self.name, self.sem, self.step = name, sem, step
        self.total = 0
        self.last_op = None
        self.pending = []


class Buf:
    __slots__ = ("name", "last_w", "readers", "dma", "multi", "writers")

    def __init__(self, name, dma=None, multi=False):
        self.name, self.last_w, self.readers, self.dma = name, None, {}, dma
        self.multi, self.writers = multi, {}


class Op:
    __slots__ = ("eng", "dom", "emit", "deps", "signal", "sigval", "seq", "emitted", "next_sig", "tag")

    def __init__(self, eng, dom, emit, seq):
        self.eng, self.dom, self.emit, self.seq = eng, dom, emit, seq
        self.deps, self.signal, self.sigval = [], False, None
        self.emitted, self.next_sig = False, None


ENGS = ("pe", "act", "dve", "pool", "sp")


class Sched:
    def __init__(self, nc, n_dma_sems=96, n_cc_sems=0):
        self.nc = nc
        self.cdom = {e: Dom(e, nc.alloc_semaphore("sem_" + e), 1)
                     for e in ("pe", "act", "dve", "pool")}
        self.dma_pool = [Dom("dma%d" % i, nc.alloc_semaphore("semd%d" % i), 16)
                         for i in range(n_dma_sems)]
        n_sw = 9
        self.sw_pool, self.hw_pool = self.dma_pool[:n_sw], self.dma_pool[n_sw:]
        self.dma_next = {"sw": 0, "hw": 0}
        self.ops = {e: [] for e in ENGS}
        self.waited = {e: {} for e in ENGS}
        self.barrier_deps = {e: [] for e in ENGS}
        self.cc_doms = [Dom("cc%d" % i, nc.alloc_semaphore("semcc%d" % i), 1) for i in range(n_cc_sems)]
        self.cc_next = 0
        self.all_doms = list(self.cdom.values()) + self.dma_pool + self.cc_doms
        self.seq = 0
        self.n_inst = 0

    tag = ""

    def buf(self, name, dma=False, multi=False):
        d = None
        if dma:
            kind = "sw" if dma == "sw" else "hw"
            pool = self.sw_pool if kind == "sw" else self.hw_pool
            d = pool[self.dma_next[kind] % len(pool)]
            self.dma_next[kind] += 1
        return Buf(name, d, multi)

    @staticmethod
    def _res(o):
        if o.emitted and not o.signal:
            o = o.next_sig
        return o

    def _adddep(self, deps, o):
        o = self._res(o)
        cur = deps.get(o.dom)
        if cur is None or cur.seq < o.seq:
            deps[o.dom] = o

    def _record(self, eng, dom, emit, reads, writes, is_dma):
        self.seq += 1
        op = Op(eng, dom, emit, self.seq)
        op.tag = self.tag
        deps = {}
        for b in reads:
            if b.multi:
                for o in b.writers.values():
                    self._adddep(deps, o)
            elif b.last_w is not None:
                self._adddep(deps, b.last_w)
        for b in writes:
            if not b.multi and b.last_w is not None and b.last_w.dom is not dom:
                self._adddep(deps, b.last_w)
            for d, o in b.readers.items():
                if d is not dom:
                    self._adddep(deps, o)
        if is_dma:
            op.signal = True
            if dom.last_op is not None:
                self._adddep(deps, dom.last_op)
        for o in self.barrier_deps[eng]:
            self._adddep(deps, o)
        self.barrier_deps[eng] = []
        for o in deps.values():
            o.signal = True
        op.deps = list(deps.values())
        for b in reads:
            b.readers[dom] = op
        for b in writes:
            if b.multi:
                if b.readers:
                    b.writers = {}
                b.writers[dom] = op
            b.last_w = op
            b.readers = {}
        self.ops[eng].append(op)
        dom.last_op = op
        dom.pending.append(op)
        return op

    def op(self, eng, emit, reads=(), writes=()):
        return self._record(eng, self.cdom[eng], emit, reads, writes, False)

    def dma(self, queue, emit, buf, reads=(), writes=()):
        return self._record(queue, buf.dma, emit, reads, writes, True)

    def flush(self, final=False):
        nc = self.nc
        lasts = []
        for d in self.all_doms:
            if d.pending:
                d.pending[-1].signal = True
                nxt = None
                for o in reversed(d.pending):
                    if o.signal:
                        nxt = o
                    o.next_sig = nxt
                for o in d.pending:
                    if o.signal:
                        d.total += d.step
                        o.sigval = d.total
                d.pending = []
            if d.last_op is not None:
                lasts.append(d.last_op)
        streams = {e: self.ops[e] for e in ENGS}
        self.ops = {e: [] for e in ENGS}
        sched = self

        def run(ename, eng):
            w = sched.waited[ename]
            for o in streams[ename]:
                for dep in o.deps:
                    dep = sched._res(dep)
                    if w.get(dep.dom, 0) < dep.sigval:
                        eng.wait_ge(dep.dom.sem, dep.sigval)
                        w[dep.dom] = dep.sigval
                inst = o.emit(eng)
                sched.n_inst += 1
                if o.signal:
                    inst.then_inc(o.dom.sem, o.dom.step)
                o.emitted = True
            if final and ename == "sp":
                for d in sched.all_doms:
                    if d.total > 0 and w.get(d, 0) < d.total:
                        eng.wait_ge(d.sem, d.total)
                        w[d] = d.total

        with nc.Block() as block:
            @block.tensor
            def _(e):
                run("pe", e)

            @block.scalar
            def _(e):
                run("act", e)

            @block.vector
            def _(e):
                run("dve", e)

            @block.gpsimd
            def _(e):
                run("pool", e)

            @block.sync
            def _(e):
                run("sp", e)
        for e in ENGS:
            self.barrier_deps[e] = list(lasts)


def swpipe(n, stages):
    hi = max(o for o, _ in stages)
    lo = min(o for o, _ in stages)
    for i in range(-hi, n - lo):
        for off, fn in stages:
            k = i + off
            if 0 <= k < n:
                Sched.tag = "%s(%d)" % (fn.__name__, k)
                fn(k)


class T:
    __slots__ = ("t", "b")

    def __init__(self, t, b):
        self.t, self.b = t, b


class TS4:
    __slots__ = ("t", "bs")

    def __init__(self, t, bs):
        self.t, self.bs = t, bs


class Ring:
    def __init__(self, items):
        self.items, self.i = items, 0

    def next(self):
        x = self.items[self.i % len(self.items)]
        self.i += 1
        return x


class Builder:
    def __init__(self, nc, n_dma_sems=96, n_cc_sems=0):
        self.nc = nc
        self.S = Sched(nc, n_dma_sems, n_cc_sems)
        self.dram = {}
        self.dbuf = {}
        self.psum = None
        self.nuniq = 0

    def din(self, name, shape, dt=F32):
        self.dram[name] = self.nc.dram_tensor(name, list(shape), dt, kind="ExternalInput").ap()
        self.dbuf[name] = self.S.buf("d_" + name, multi=True)
        return self.dram[name]

    def dout(self, name, shape, dt=F32):
        self.dram[name] = self.nc.dram_tensor(name, list(shape), dt, kind="ExternalOutput").ap()
        self.dbuf[name] = self.S.buf("d_" + name, multi=True)
        return self.dram[name]

    def dint(self, name, shape, dt=F32):
        self.dram[name] = self.nc.dram_tensor(name, list(shape), dt, kind="Internal").ap()
        self.dbuf[name] = self.S.buf("d_" + name, multi=True)
        return self.dram[name]

    def sb(self, es, name, shape, dt, dma=False):
        self.nuniq += 1
        t = es.enter_context(self.nc.sbuf_tensor("%s_%d" % (name, self.nuniq), list(shape), dt))
        return T(t, self.S.buf(name, dma=dma))

    def alloc_psum(self, es):
        self.psum = []
        for i in range(8):
            t = es.enter_context(self.nc.psum_tensor("ps%d" % i, [128, 512], F32))
            self.psum.append(T(t, self.S.buf("ps%d" % i)))

    def mm(self, out, outb, lhsT, lb, rhs, rb, start, stop):
        self.S.op("pe", lambda e: e.matmul(out, lhsT=lhsT, rhs=rhs, start=start, stop=stop),
                  reads=[lb, rb], writes=[outb])

    def act(self, out, outb, in_, inb, func, scale=1.0, bias=None, extra_reads=()):
        if bias is None:
            self.S.op("act", lambda e: e.activation(out=out, in_=in_, func=func, scale=scale),
                      reads=[inb, *extra_reads], writes=[outb])
        else:
            self.S.op("act", lambda e: e.activation(out=out, in_=in_, func=func, scale=scale, bias=bias),
                      reads=[inb, *extra_reads], writes=[outb])

    def tt(self, eng, out, outb, in0, b0, in1, b1, op):
        self.S.op(eng, lambda e: e.tensor_tensor(out=out, in0=in0, in1=in1, op=op),
                  reads=[b0, b1], writes=[outb])

    def stt(self, eng, out, outb, in0, b0, scalar, sb_, in1, b1, op0, op1):
        rd = [b0, b1] + ([sb_] if sb_ is not None else [])
        self.S.op(eng, lambda e: e.scalar_tensor_tensor(out=out, in0=in0, scalar=scalar, in1=in1,
                                                        op0=op0, op1=op1),
                  reads=rd, writes=[outb])

    def ts(self, eng, out, outb, in0, b0, s1, s2, op0, op1=None, sbufs=()):
        if op1 is None:
            self.S.op(eng, lambda e: e.tensor_scalar(out=out, in0=in0, scalar1=s1, scalar2=None, op0=op0),
                      reads=[b0, *sbufs], writes=[outb])
        else:
            self.S.op(eng, lambda e: e.tensor_scalar(out=out, in0=in0, scalar1=s1, scalar2=s2,
                                                     op0=op0, op1=op1),
                      reads=[b0, *sbufs], writes=[outb])

    def copy(self, eng, out, outb, in_, inb):
        if eng == "act":
            self.act(out, outb, in_, inb, AF.Copy)
        else:
            self.S.op(eng, lambda e: e.tensor_copy(out=out, in_=in_), reads=[inb], writes=[outb])

    def memset(self, eng, out, outb, val):
        self.S.op(eng, lambda e: e.memset(out, val), writes=[outb])

    def load(self, q, dst, dstT, src, srcname):
        self.S.dma(q, lambda e: e.dma_start(out=dst, in_=src), dstT.b,
                   reads=[self.dbuf[srcname]], writes=[dstT.b])

    def store(self, q, dst, dstname, src, srcT):
        self.S.dma(q, lambda e: e.dma_start(out=dst, in_=src), srcT.b,
                   reads=[srcT.b], writes=[self.dbuf[dstname]])

    def tok_alloc(self, es, need_attn=False):
        b = self
        L = {}
        L["hring"] = Ring([b.sb(es, "hsb%d" % i, [128, DC, TS], F32, dma=True) for i in range(2)])
        L["h"] = None
        L["u"] = b.sb(es, "usb", [128, DC, TS], BF16)
        L["ub"] = [b.S.buf("u%d" % i) for i in range(DC)]
        L["a"] = b.sb(es, "asb", [128, FC, TS], BF16)
        L["sq"] = Ring([b.sb(es, "sq%d" % i, [128, TS], BF16) for i in range(4)])
        L["rstd"] = b.sb(es, "rstd", [128, TS], F32)
        L["sg"] = Ring([b.sb(es, "sg%d" % i, [128, TS], F32) for i in range(2)])
        L["wa"] = Ring([b.sb(es, "wa%d" % i, [128, DC, 128], BF16, dma="sw") for i in range(4)])
        L["wd"] = Ring([b.sb(es, "wd%d" % i, [128, FC * 128], BF16, dma="sw") for i in range(3)])
        L["gains"] = b.sb(es, "gains", [128, 7 * DC], F32, dma=True)
        L["onesD"] = b.sb(es, "onesD", [128, 128], BF16)
        L["ev"] = Ring([b.sb(es, "ev%d" % i, [128, TS], BF16, dma=True) for i in range(4)])
        b.memset("pool", L["onesD"].t[:], L["onesD"].b, 1.0 / D)
        b.load("sp", L["gains"].t[:], L["gains"], b.dram["gains"].rearrange("p k c -> p (k c)"), "gains")
        L["pg"] = Ring([b.psum[0], b.psum[1]])
        L["pu"] = Ring([b.psum[2], b.psum[3]])
        L["pd"] = Ring([b.psum[4], b.psum[5]])
        L["pn"] = b.psum[6]
        if need_attn:
            L["attn"] = b.sb(es, "attn_t", [128, NH, TS], BF16, dma=True)
        L["o32"] = Ring([b.sb(es, "o32_%d" % i, [128, TS], F32, dma=True) for i in range(2)])
        return L

    def load_h(self, L, src, j):
        hb = L["hring"].next()
        d = self.dram[src].rearrange("c p t -> p c t")[:, :, j * TS:(j + 1) * TS]
        for c0 in range(0, DC, 4):
            self.load("sp", hb.t[:, c0:c0 + 4, :], hb, d[:, c0:c0 + 4, :], src)
        return hb

    def store_h(self, L, dst, j):
        hb = L["h"]
        d = self.dram[dst].rearrange("c p t -> p c t")[:, :, j * TS:(j + 1) * TS]
        for c0 in range(0, DC, 4):
            self.store("sp", d[:, c0:c0 + 4, :], dst, hb.t[:, c0:c0 + 4, :], hb)

    def rmsnorm(self, L, gidx, out32=None):
        b = self
        h, pn = L["h"], L["pn"]
        for dc in range(DC):
            sq = L["sq"].next()
            if dc % 2 == 0:
                b.act(sq.t[:], sq.b, h.t[:, dc, :], h.b, AF.Square)
            else:
                b.tt("dve", sq.t[:], sq.b, h.t[:, dc, :], h.b, h.t[:, dc, :], h.b, ALU.mult)
            b.mm(pn.t[:], pn.b, L["onesD"].t[:], L["onesD"].b, sq.t[:], sq.b, dc == 0, dc == DC - 1)
        r = L["rstd"]
        b.act(r.t[:], r.b, pn.t[:], pn.b, AF.Sqrt, bias=EPS)
        b.S.op("dve", lambda e: e.reciprocal(out=r.t[:], in_=r.t[:]), reads=[r.b], writes=[r.b])
        g = L["gains"]
        for dc in range(DC):
            col = gidx * DC + dc
            if out32 is None:
                dst = L["u"]
                b.stt("dve", dst.t[:, dc, :], L["ub"][dc], h.t[:, dc, :], h.b, g.t[:, col:col + 1], g.b,
                      r.t[:], r.b, ALU.mult, ALU.mult)
            else:
                name, j = out32
                o = L["o32"].next()
                b.stt("dve", o.t[:], o.b, h.t[:, dc, :], h.b, g.t[:, col:col + 1], g.b,
                      r.t[:], r.b, ALU.mult, ALU.mult)
                b.store("sp", b.dram[name][dc][:, j * TS:(j + 1) * TS], name, o.t[:], o)

    def load_wa(self, L, name, idx):
        w = L["wa"].next()
        self.load("pool", w.t[:], w, self.dram[name][idx], name)
        return w

    def ffn(self, L, f):
        b = self
        u, a, h = L["u"], L["a"], L["h"]
        for c in range(FC):
            wg = b.load_wa(L, "wg%d" % f, c)
            wu = b.load_wa(L, "wu%d" % f, c)
            pg, pu = L["pg"].next(), L["pu"].next()
            for dc in range(DC):
                b.mm(pg.t[:], pg.b, wg.t[:, dc, :], wg.b, u.t[:, dc, :], L["ub"][dc], dc == 0, dc == DC - 1)
            for dc in range(DC):
                b.mm(pu.t[:], pu.b, wu.t[:, dc, :], wu.b, u.t[:, dc, :], L["ub"][dc], dc == 0, dc == DC - 1)
            sg = L["sg"].next()
            b.act(sg.t[:], sg.b, pg.t[:], pg.b, AF.Silu)
            b.tt("dve", a.t[:, c, :], a.b, sg.t[:], sg.b, pu.t[:], pu.b, ALU.mult)
        for dmc in range(DC):
            w = L["wd"].next()
            wv = w.t[:, 0:FC * 128].rearrange("p (c m) -> p c m", m=128)
            b.load("pool", wv, w, b.dram["wd%d" % f][dmc], "wd%d" % f)
            pd = L["pd"].next()
            for c in range(FC):
                b.mm(pd.t[:], pd.b, wv[:, c, :], w.b, a.t[:, c, :], a.b, c == 0, c == FC - 1)
            b.stt("dve", h.t[:, dmc, :], h.b, pd.t[:], pd.b, 0.5, None, h.t[:, dmc, :], h.b,
                  ALU.mult, ALU.add)

    def qkv(self, L, l, j, km=None):
        import os
        b = self
        u = L["u"]
        dbg = os.environ.get("QKV_DBG", "")
        if "nokm" in dbg:
            km = None
        pq = Ring([b.psum[0], b.psum[1], b.psum[2], b.psum[3]])
        for un in range(0 if "noqk" not in dbg else 32, 32):
            w = b.load_wa(L, "wqk%d" % l, un)
            p = pq.next()
            for dc in range(DC):
                b.mm(p.t[:], p.b, w.t[:, dc, :], w.b, u.t[:, dc, :], L["ub"][dc], dc == 0, dc == DC - 1)
            ev = L["ev"].next()
            b.copy("dve", ev.t[:], ev.b, p.t[:], p.b)
            if un < 16:
                b.store("sp", b.dram["qT"][un][:, j * TS:(j + 1) * TS], "qT", ev.t[:], ev)
            else:
                nm, ap = b.k_dst(un - 16, j)
                b.store("sp", ap, nm, ev.t[:], ev)
            if km is not None and un >= 24:
                hm = un - 24
                b.S.op("dve", lambda e, hm=hm, p=p: e.reduce_sum(
                    out=km.t[:, hm, 2 * j:2 * j + 2],
                    in_=p.t[:].rearrange("p (b k) -> p b k", b=2), axis=AX.X),
                    reads=[p.b], writes=[km.b])
        pv = Ring([b.psum[4], b.psum[5]])
        for g in range(4):
            ws = []
            for hf in range(2):
                w = L["wd"].next()
                wv = w.t[:, 0:DC * 256].rearrange("p (c n) -> p c n", n=256)
                b.load("pool", wv, w, b.dram["wv%d" % l][2 * g + hf], "wv%d" % l)
                ws.append((w, wv))
            for sub in range(4):
                p = pv.next()
                for hf in range(2):
                    w, wv = ws[hf]
                    for dc in range(DC):
                        b.mm(p.t[:, hf * 256:(hf + 1) * 256], p.b, u.t[:, dc, sub * 128:(sub + 1) * 128],
                             L["ub"][dc], wv[:, dc, :], w.b, dc == 0, dc == DC - 1)
                ev = L["ev"].next()
                b.copy("dve", ev.t[:], ev.b, p.t[:], p.b)
                nm, ap = b.v_dst(j, sub, g)
                b.store("sp", ap, nm, ev.t[:], ev)

    def outproj(self, L, l, j):
        b = self
        h = L["h"]
        attn = L["attn"]
        asrc = b.dram["attnD"].rearrange("h p t -> p h t")[:, :, j * TS:(j + 1) * TS]
        for c0 in range(0, NH, 4):
            b.load("sp", attn.t[:, c0:c0 + 4, :], attn, asrc[:, c0:c0 + 4, :], "attnD")
        for dmc in range(DC):
            w = b.load_wa(L, "wo%d" % l, dmc)
            pd = L["pd"].next()
            for hc in range(NH):
                b.mm(pd.t[:], pd.b, w.t[:, hc, :], w.b, attn.t[:, hc, :], attn.b,
                     hc == 0, hc == NH - 1)
            b.tt("dve", h.t[:, dmc, :], h.b, pd.t[:], pd.b, h.t[:, dmc, :], h.b, ALU.add)

    def attn_alloc(self, es):
        b = self
        A = {}
        def sb4(name, shape):
            t = b.sb(es, name, shape, BF16)
            return TS4(t.t, [b.S.buf("%s_%d" % (name, a), dma=True) for a in range(4)])
        A["kT"] = Ring([sb4("kT%d" % i, [128, SEQ]) for i in range(2)])
        A["v"] = Ring([sb4("v%d" % i, [128, 32, HD]) for i in range(2)])
        A["q"] = Ring([b.sb(es, "q%d" % i, [128, NTOK], BF16, dma=True) for i in range(2)])
        A["e"] = Ring([b.sb(es, "e%d" % i, [128, TS], F32) for i in range(6)])
        A["p"] = Ring([b.sb(es, "p%d" % i, [128, TS], BF16) for i in range(4)])
        A["ao"] = Ring([b.sb(es, "ao%d" % i, [128, TS], BF16, dma=True) for i in range(2)])
        A["cst"] = b.sb(es, "cst", [128, CSTW], BF16, dma="sw")
        b.load("pool", A["cst"].t[:], A["cst"], b.dram["cstb"], "cstb")
        return A

    def load_head(self, A, h):
        b = self
        kT, v, q = A["kT"].next(), A["v"].next(), A["q"].next()
        def ld(dst, buf, src, nm):
            b.S.dma("sp", lambda e: e.dma_start(out=dst, in_=src), buf, reads=[b.dbuf[nm]], writes=[buf])

        if b.fused:
            for a in range(4):
                for r in range(2):
                    Tt = 2 * a + r
                    nm = "kTg%d" % a
                    ld(kT.t[:, Tt * TS:(Tt + 1) * TS], kT.bs[a],
                       b.dram[nm][r * 2048 + h * HD:r * 2048 + (h + 1) * HD, :], nm)
            for a in range(4):
                for r in range(2):
                    Tt = 2 * a + r
                    nm = "vg%d" % a
                    srcv = b.dram[nm][r * TS:(r + 1) * TS, h * HD:(h + 1) * HD].rearrange("(q p) d -> p q d", p=128)
                    ld(v.t[:, Tt * 4:(Tt + 1) * 4, :], v.bs[a], srcv, nm)
        else:
            for r in range(2):
                for a in range(4):
                    Tt = 2 * a + r
                    src = b.dram["kTg"][r][h][:, a * TS:(a + 1) * TS]
                    ld(kT.t[:, Tt * TS:(Tt + 1) * TS], kT.bs[a], src, "kTg")
                    srcv = b.dram["vg"][r][a * TS:(a + 1) * TS, h * HD:(h + 1) * HD].rearrange(
                        "(q p) d -> p q d", p=128)
                    ld(v.t[:, Tt * 4:(Tt + 1) * 4, :], v.bs[a], srcv, "vg")
        b.load("sp", q.t[:], q, b.dram["qT"][h], "qT")
        return kT, v, q

    fused = False

    def k_dst(self, h, j):
        if self.fused:
            nm = "kTl%d" % j
            return nm, self.dram[nm][h * HD:(h + 1) * HD, :]
        return "kTl", self.dram["kTl"][h][:, j * TS:(j + 1) * TS]

    def v_dst(self, j, sub, g):
        if self.fused:
            nm = "vl%d" % j
            return nm, self.dram[nm][sub * 128:(sub + 1) * 128, g * 512:(g + 1) * 512]
        r0 = j * TS + sub * 128
        return "vl", self.dram["vl"][r0:r0 + 128, g * 512:(g + 1) * 512]

    def km_src(self, r, hm):
        if self.fused:
            return self.dram["kmg"][r * 1024 + hm * 128:r * 1024 + (hm + 1) * 128, :].rearrange("p (a c) -> p a c", c=2)
        return self.dram["kmg"][r][hm].rearrange("p (a c) -> p a c", c=2)

    def allgather(self, src, dst):
        S = self.S
        d = S.cc_doms[S.cc_next % len(S.cc_doms)]
        S.cc_next += 1
        cb = Buf("cc", d)
        si, so = self.dram[src].opt(), self.dram[dst].opt()
        S.dma("pool", lambda e: e.collective_compute("AllGather", ALU.bypass,
                                                     replica_groups=[[0, 1], [2, 3], [4, 5], [6, 7]],
                                                     ins=[si], outs=[so]),
              cb, reads=[self.dbuf[src]], writes=[self.dbuf[dst]])

    def exchange_tile(self, j):
        self.allgather("kTl%d" % j, "kTg%d" % j)
        self.allgather("vl%d" % j, "vg%d" % j)

    def attn_out(self, A, src, srcb, h, j, rden=None):
        b = self
        ao = A["ao"].next()
        if rden is None:
            b.copy("dve", ao.t[:], ao.b, src, srcb)
        else:
            b.tt("dve", ao.t[:], ao.b, src, srcb, rden.t[:], rden.b, ALU.mult)
        b.store("sp", b.dram["attnD"][h][:, j * TS:(j + 1) * TS], "attnD", ao.t[:], ao)

    def attn0(self, es):
        b = self
        A = b.attn_alloc(es)
        ones = A["cst"].t[:, 0:128]
        cb = A["cst"].b
        selrow = A["cst"].t[0:16, 256:256 + 16 * 128].rearrange("p (n m) -> p n m", m=128)
        rel = b.sb(es, "rel", [32, 16], F32, dma=True)
        erel = b.sb(es, "erel", [32, 16], F32)
        Mt = b.sb(es, "Mt", [32, 2, XL], F32, dma=True)
        fl = b.sb(es, "fl", [8, 2, XL], F32, dma=True)
        tabfar = b.sb(es, "tabfar", [128, 8], F32, dma=True)
        Jm = b.sb(es, "Jm", [128, 128], F32)
        b.load("sp", rel.t[:], rel, b.dram["rel_bias"], "rel_bias")
        b.load("sp", Mt.t[:, 0, :], Mt, b.dram["Mdil"], "Mdil")
        b.load("sp", Mt.t[:, 1, :], Mt, b.dram["Mmoba"], "Mmoba")
        b.load("sp", tabfar.t[:], tabfar, b.dram["rel_bias"][31:32, 8:16].partition_broadcast(128), "rel_bias")
        b.act(erel.t[:], erel.b, rel.t[:], rel.b, AF.Exp)
        b.memset("pool", Jm.t[:], Jm.b, 0.0)
        b.S.op("pool", lambda e: e.affine_select(out=Jm.t[:], in_=Jm.t[:], pattern=[[1, 128]],
                                                 compare_op=ALU.not_equal, fill=1.0, base=-127,
                                                 channel_multiplier=1),
               reads=[Jm.b], writes=[Jm.b])
        pm = b.psum[6]
        for k in range(2):
            for c in range(XL // 512):
                b.mm(pm.t[0:8, :], pm.b, erel.t[:, 8 * k:8 * k + 8], erel.b,
                     Mt.t[:, k, c * 512:(c + 1) * 512], Mt.b, True, True)
                b.copy("dve", fl.t[:, k, c * 512:(c + 1) * 512], fl.b, pm.t[0:8, :], pm.b)
            b.store("sp", b.dram["flat"][8 * k:8 * k + 8, :], "flat", fl.t[:, k, :], fl)
        pastneg = b.sb(es, "pastneg", [128, 256], F32, dma=True)
        notown = b.sb(es, "notown", [128, 256], F32, dma=True)
        b.load("sp", pastneg.t[:], pastneg, b.dram["pastneg"], "pastneg")
        b.load("sp", notown.t[:], notown, b.dram["notown"], "notown")
        ident = b.sb(es, "ident", [128, 128], F32)
        b.memset("pool", ident.t[:], ident.b, 0.0)
        b.S.op("pool", lambda e: e.affine_select(out=ident.t[:], in_=ident.t[:], pattern=[[-1, 128]],
                                                 compare_op=ALU.not_equal, fill=1.0, base=0,
                                                 channel_multiplier=1),
               reads=[ident.b], writes=[ident.b])
        gsp = b.sb(es, "gsp", [128, GW], F32, dma=True)
        gs_ring = Ring([b.sb(es, "gs%d" % i, [128, GW], F32) for i in range(2)])
        kmf = b.sb(es, "kmf", [128, 16], F32, dma=True)
        kmb = b.sb(es, "kmb", [128, 16], BF16)
        gm = b.sb(es, "gm", [128, 64], F32)
        mx8 = b.sb(es, "mx8", [128, 4, 8], F32)
        thr = b.sb(es, "thr", [128, 4], F32)
        nsel = b.sb(es, "nsel", [128, 64], F32)
        nselT = Ring([b.sb(es, "nselT%d" % i, [16, TS], BF16) for i in range(2)])
        rden = b.sb(es, "rden", [128, TS], F32)
        lden = b.sb(es, "lden", [128, TS], F32)
        pfar = Ring([b.sb(es, "pfar%d" % i, [128, TS], BF16) for i in range(6)])
        ghi = b.sb(es, "ghi", [128, GW], BF16)
        glo = b.sb(es, "glo", [128, GW], BF16)
        Jb = b.sb(es, "Jb", [128, 128], BF16)
        b.copy("pool", Jb.t[:], Jb.b, Jm.t[:], Jm.b)
        pj_ring = Ring([b.psum[6], b.psum[7]])
        ps_ring = Ring([b.psum[0], b.psum[1]])
        po_ring = Ring([b.psum[2], b.psum[3]])
        pden_ring = Ring([b.psum[4], b.psum[5]])
        mult_eng = Ring(["dve"])

        def prep_load(h):
            hd = b.load_head(A, h)
            src = bass.AP(b.dram["flat"].tensor, h * XL, [[1, 128], [1, GW]])
            b.load("sp", gsp.t[:], gsp, src, "flat")
            b.copy("pool", ghi.t[:], ghi.b, gsp.t[:], gsp.b)
            b.tt("pool", glo.t[:], glo.b, gsp.t[:], gsp.b, ghi.t[:], ghi.b, ALU.subtract)
            return hd

        def prep_table(hd):
            gs = gs_ring.next()
            for c in range((GW + 511) // 512):
                w = min(512, GW - c * 512)
                pj = pj_ring.next()
                b.mm(pj.t[:, 0:w], pj.b, Jb.t[:], Jb.b, ghi.t[:, c * 512:c * 512 + w], ghi.b, True, False)
                b.mm(pj.t[:, 0:w], pj.b, Jb.t[:], Jb.b, glo.t[:, c * 512:c * 512 + w], glo.b, False, True)
                b.copy("dve", gs.t[:, c * 512:c * 512 + w], gs.b, pj.t[:, 0:w], pj.b)
            return hd, gs

        def load_km(h):
            hm = h - 8
            for r in range(2):
                dst = kmf.t[:].rearrange("p (a r c) -> p a r c", r=2, c=2)[:, :, r, :]
                b.load("sp", dst, kmf, b.km_src(r, hm), "kmg")
            b.copy("dve", kmb.t[:], kmb.b, kmf.t[:], kmf.b)

        def gate(h, j, q):
            pg = b.psum[7]
            for sub in range(4):
                b.mm(pg.t[:, sub * 16:(sub + 1) * 16], pg.b,
                     q.t[:, j * TS + sub * 128:j * TS + (sub + 1) * 128], q.b, kmb.t[:], kmb.b, True, True)
            b.tt("dve", gm.t[:], gm.b, pg.t[:, 0:64], pg.b, pastneg.t[:, j * 64:(j + 1) * 64], pastneg.b, ALU.add)
            for sub in range(4):
                b.S.op("dve", lambda e, sub=sub: e.max(out=mx8.t[:, sub, :], in_=gm.t[:, sub * 16:(sub + 1) * 16]),
                       reads=[gm.b], writes=[mx8.b])
            b.ts("dve", thr.t[:], thr.b, mx8.t[:, :, 2], mx8.b, -1e29, None, ALU.max)
            for sub in range(4):
                b.ts("dve", nsel.t[:, sub * 16:(sub + 1) * 16], nsel.b, gm.t[:, sub * 16:(sub + 1) * 16], gm.b,
                     thr.t[:, sub:sub + 1], None, ALU.is_lt, sbufs=[thr.b])
            b.tt("dve", nsel.t[:], nsel.b, nsel.t[:], nsel.b, notown.t[:, j * 64:(j + 1) * 64], notown.b, ALU.mult)
            for sub in range(4):
                b.S.op("pe", lambda e, sub=sub: e.transpose(b.psum[6].t[0:16, sub * 128:(sub + 1) * 128],
                                                            nsel.t[:, sub * 16:(sub + 1) * 16], ident.t[:]),
                       reads=[nsel.b, ident.b], writes=[b.psum[6].b])
            nT = nselT.next()
            b.copy("dve", nT.t[:], nT.b, b.psum[6].t[0:16, :], b.psum[6].b)
            return nT

        steps = []
        for h in range(NH):
            for j in range(NT):
                kb_lo = 0 if h >= 8 else max(0, 8 * j - 16)
                kbs = list(range(kb_lo, 8 * j + 8))
                for k, kb in enumerate(kbs):
                    steps.append((h, j, k, kb, len(kbs)))
        G = len(steps)
        st = [dict() for _ in range(G)]
        heads = {0: prep_table(prep_load(0))}
        pend = {}
        tiles = {}

        def fS(g):
            h, j, k, kb, n = steps[g]
            moba = h >= 8
            if j == 0 and k == 6 and h + 1 < NH:
                pend[h + 1] = prep_load(h + 1)
            if j == 2 and k == 0 and h + 1 < NH:
                heads[h + 1] = prep_table(pend.pop(h + 1))
            (kT, v, q), gs = heads[h]
            if k == 0:
                tiles.setdefault((h, j), {})
                if g == 0 and moba:
                    load_km(h)
                    tiles[(h, j)]["nT"] = gate(h, j, q)
                h2, j2 = (h, j + 1) if j + 1 < NT else (h + 1, 0)
                if h2 < NH and h2 >= 8:
                    if j2 == 0:
                        load_km(h2)
                    tiles.setdefault((h2, j2), {})["nT"] = gate(h2, j2, heads[h2][0][2])
            ps = ps_ring.next()
            b.mm(ps.t[:], ps.b, kT.t[:, kb * 128:(kb + 1) * 128], kT.bs[kb // 8], q.t[:, j * TS:(j + 1) * TS], q.b,
                 True, not moba)
            if moba:
                nT = tiles[(h, j)]["nT"]
                b.mm(ps.t[:], ps.b, selrow[:, kb // 2, :], cb, nT.t[:], nT.b, False, True)
            st[g]["ps"] = ps

        def fE(g):
            h, j, k, kb, n = steps[g]
            moba = h >= 8
            ps = st[g].pop("ps")
            delta = (8 * j - kb) * 128
            far = moba and delta >= 1664
            st[g]["delta"] = delta
            if far:
                p = pfar.next()
                b.act(p.t[:], p.b, ps.t[:], ps.b, AF.Exp, scale=SCALE,
                      bias=tabfar.t[:, h - 8:h - 7], extra_reads=[tabfar.b])
                st[g]["p"] = p
            else:
                e_ = A["e"].next()
                b.act(e_.t[:], e_.b, ps.t[:], ps.b, AF.Exp, scale=SCALE)
                st[g]["e"] = e_

        def fP(g):
            h = steps[g][0]
            if "e" in st[g]:
                gs = heads[h][1]
                e_ = st[g].pop("e")
                p = A["p"].next()
                i0 = st[g]["delta"] + 896
                b.tt(mult_eng.next(), p.t[:], p.b, e_.t[:], e_.b, gs.t[:, i0:i0 + TS], gs.b, ALU.mult)
                st[g]["p"] = p

        def fPV(g):
            h, j, k, kb, n = steps[g]
            tl = tiles[(h, j)]
            if k == 0:
                tl["po"], tl["pden"] = po_ring.next(), pden_ring.next()
            po, pden = tl["po"], tl["pden"]
            v = heads[h][0][1]
            p = st[g].pop("p")
            b.mm(po.t[:], po.b, v.t[:, kb, :], v.bs[kb // 8], p.t[:], p.b, k == 0, k == n - 1)
            b.mm(pden.t[:], pden.b, ones, cb, p.t[:], p.b, k == 0, k == n - 1)
            if k == n - 1:
                b.act(lden.t[:], lden.b, pden.t[:], pden.b, AF.Ln)
                b.act(rden.t[:], rden.b, lden.t[:], lden.b, AF.Exp, scale=-1.0)
                b.attn_out(A, po.t[:], po.b, h, j, rden=rden)

        swpipe(G, [(3, fS), (3, fE), (1, fP), (0, fPV)])

    def attn1(self, es):
        b = self
        A = b.attn_alloc(es)
        ones = A["cst"].t[:, 0:128]
        tri = A["cst"].t[:, 128:256]
        cb = A["cst"].b
        tri2 = A["cst"].t[:, 256 + 2048:256 + 2048 + 128]
        sbm = b.sb(es, "sbm", [128, 8, TS], BF16, dma="sw")
        b.load("pool", sbm.t[:], sbm, b.dram["sbmask"], "sbmask")
        Sb_ring = Ring([b.sb(es, "Sb%d" % i, [128, TS], BF16) for i in range(5)])
        sp_ring = Ring([b.sb(es, "sp%d" % i, [128, TS], BF16) for i in range(4)])
        ew_ring = Ring([b.sb(es, "ew%d" % i, [128, TS], F32) for i in range(2)])
        pz_ring = Ring([b.psum[0], b.psum[1]])
        pw_ring = Ring([b.psum[2], b.psum[3], b.psum[4]])
        po_ring = Ring([b.psum[5], b.psum[6]])
        steps = []
        for h in range(NH):
            for j in range(NT):
                kbs = list(range(8 * j + 7, -1, -1))
                for k, kb in enumerate(kbs):
                    steps.append((h, j, k, kb, len(kbs)))
        G = len(steps)
        st = [dict() for _ in range(G)]
        heads = {0: b.load_head(A, 0)}
        pos = {}

        def fZ(g):
            h, j, k, kb, n = steps[g]
            if j == 0 and k == 6 and h + 1 < NH:
                heads[h + 1] = b.load_head(A, h + 1)
            kT, v, q = heads[h]
            pz = pz_ring.next()
            diag = kb >= 8 * j
            b.mm(pz.t[:], pz.b, kT.t[:, kb * 128:(kb + 1) * 128], kT.bs[kb // 8], q.t[:, j * TS:(j + 1) * TS], q.b,
                 True, not diag)
            if diag:
                b.mm(pz.t[:], pz.b, tri2, cb, sbm.t[:, kb - 8 * j, :], sbm.b, False, True)
            st[g]["pz"] = pz

        def fE(g):
            pz = st[g].pop("pz")
            e_ = A["e"].next()
            b.act(e_.t[:], e_.b, pz.t[:], pz.b, AF.Exp, scale=SCALE)
            st[g]["e"] = e_

        def fSP(g):
            e_ = st[g]["e"]
            sp = sp_ring.next()
            b.act(sp.t[:], sp.b, e_.t[:], e_.b, AF.Ln, bias=1.0)
            st[g]["sp"] = sp

        def fPW(g):
            k = steps[g][2]
            sp = st[g]["sp"]
            pw = pw_ring.next()
            b.mm(pw.t[:], pw.b, tri, cb, sp.t[:], sp.b, True, k == 0)
            if k > 0:
                Sp = st[g - 1]["Sb"]
                b.mm(pw.t[:], pw.b, ones, cb, Sp.t[:], Sp.b, False, True)
            st[g]["pw"] = pw

        def fS(g):
            k = steps[g][2]
            sp = st[g]["sp"]
            Sn = Sb_ring.next()
            if k == 0:
                b.copy("dve", Sn.t[:], Sn.b, sp.t[:], sp.b)
            else:
                Sp = st[g - 1]["Sb"]
                b.tt("dve", Sn.t[:], Sn.b, Sp.t[:], Sp.b, sp.t[:], sp.b, ALU.add)
            st[g]["Sb"] = Sn

        def fEW(g):
            pw = st[g].pop("pw")
            ew = ew_ring.next()
            b.act(ew.t[:], ew.b, pw.t[:], pw.b, AF.Exp, scale=-1.0)
            st[g]["ew"] = ew

        def fA(g):
            e_, ew = st[g].pop("e"), st[g].pop("ew")
            a_ = A["p"].next()
            b.tt("dve", a_.t[:], a_.b, e_.t[:], e_.b, ew.t[:], ew.b, ALU.mult)
            st[g]["a"] = a_

        def fPV(g):
            h, j, k, kb, n = steps[g]
            a_ = st[g].pop("a")
            if k == 0:
                pos[(h, j)] = po_ring.next()
            po = pos[(h, j)]
            v = heads[h][1]
            b.mm(po.t[:], po.b, v.t[:, kb, :], v.bs[kb // 8], a_.t[:], a_.b, k == 0, k == n - 1)
            if k == n - 1:
                b.attn_out(A, po.t[:], po.b, h, j)
                if g >= 8:
                    st[g - 8].clear()

        swpipe(G, [(3, fZ), (3, fE), (0, fEW), (0, fA), (3, fSP), (1, fPW), (3, fS), (-2, fPV)])


def decl_ffn_w(b, f):
    b.din("wg%d" % f, [FC, 128, DC, 128])
    b.din("wu%d" % f, [FC, 128, DC, 128])
    b.din("wd%d" % f, [DC, 128, FC, 128])


def decl_qkv_w(b, l):
    b.din("wqk%d" % l, [32, 128, DC, 128])
    b.din("wv%d" % l, [8, 128, DC, 256])


def build_L1(nt=NT, level=9):
    nc = bass.Bass("TRN2", target_bir_lowering=False)
    b = Builder(nc)
    b.din("xT", [DC, 128, NTOK])
    b.din("gains", [128, 7, DC])
    decl_ffn_w(b, 0)
    decl_qkv_w(b, 0)
    b.dout("hT", [DC, 128, NTOK])
    b.dout("qT", [NH, 128, NTOK], BF16)
    b.dout("kTl", [NH, 128, NTOK], BF16)
    b.dout("vl", [NTOK, D], BF16)
    b.dout("kml", [8, 128, 8])
    with ExitStack() as es:
        b.alloc_psum(es)
        L = b.tok_alloc(es)
        km = b.sb(es, "km", [128, 8, 8], F32, dma=True)
        for j in range(nt):
            b.load_h(L, "xT", j)
            if level >= 1:
                b.rmsnorm(L, 0)
            if level >= 2:
                b.ffn(L, 0)
            if level >= 3:
                b.rmsnorm(L, 1)
                b.qkv(L, 0, j, km=km)
            b.store_h(L, "hT", j)
        if level >= 3:
            b.ts("dve", km.t[:], km.b, km.t[:], km.b, 1.0 / 256.0, None, ALU.mult)
            b.store("sp", b.dram["kml"].rearrange("h p n -> p h n"), "kml", km.t[:], km)
        b.S.flush(final=True)
    return nc, b


def decl_attn_in(b, layer0, attn_only=False):
    if not attn_only:
        b.din("hTin", [DC, 128, NTOK])
    b.din("qT", [NH, 128, NTOK], BF16)
    b.din("kTg", [2, NH, 128, NTOK], BF16)
    b.din("vg", [2, NTOK, D], BF16)
    b.din("cstb", [128, CSTW])
    if layer0:
        b.din("kmg", [2, 8, 128, 8])
        b.din("rel_bias", [32, 16])
        b.din("Mdil", [32, XL])
        b.din("Mmoba", [32, XL])
        b.din("pastneg", [128, 256])
        b.din("notown", [128, 256])
        b.dint("flat", [16, XL])
    else:
        b.din("sbmask", [128, 8, TS])


def build_L2(nt=NT, attn_only=False):
    nc = bass.Bass("TRN2", target_bir_lowering=False)
    b = Builder(nc)
    decl_attn_in(b, True, attn_only)
    if not attn_only:
        b.din("gains", [128, 7, DC])
        b.din("wo0", [DC, 128, NH, 128])
        decl_ffn_w(b, 1)
        decl_ffn_w(b, 2)
        decl_qkv_w(b, 1)
        b.dout("hT", [DC, 128, NTOK])
        b.dout("qTo", [NH, 128, NTOK], BF16)
        b.dout("kTl", [NH, 128, NTOK], BF16)
        b.dout("vl", [NTOK, D], BF16)
    if attn_only:
        b.dout("attn_dbg", [NH, 128, NTOK], BF16)
    if attn_only:
        b.dram["attnD"], b.dbuf["attnD"] = b.dram["attn_dbg"], b.dbuf["attn_dbg"]
    else:
        b.dint("attnD", [NH, 128, NTOK], BF16)
    with ExitStack() as es0:
        b.alloc_psum(es0)
        with ExitStack() as es:
            b.attn0(es)
            b.S.flush(final=attn_only)
        with ExitStack() as es:
            if not attn_only:
                L = b.tok_alloc(es, need_attn=True)
                b.dram["qT_in"] = b.dram["qT"]
                b.dram["qT"] = b.dram["qTo"]
                b.dbuf["qT"] = b.dbuf["qTo"]
                for j in range(nt):
                    b.load_h(L, "hTin", j)
                    b.outproj(L, 0, j)
                    b.rmsnorm(L, 2)
                    b.ffn(L, 1)
                    b.rmsnorm(L, 3)
                    b.ffn(L, 2)
                    b.rmsnorm(L, 4)
                    b.qkv(L, 1, j)
                    b.store_h(L, "hT", j)
                b.S.flush(final=True)
    return nc, b


def build_L3(nt=NT, attn_only=False):
    nc = bass.Bass("TRN2", target_bir_lowering=False)
    b = Builder(nc)
    decl_attn_in(b, False, attn_only)
    if not attn_only:
        b.din("gains", [128, 7, DC])
        b.din("wo1", [DC, 128, NH, 128])
        decl_ffn_w(b, 3)
        b.dout("outT", [DC, 128, NTOK])
    if attn_only:
        b.dout("attn_dbg", [NH, 128, NTOK], BF16)
    if attn_only:
        b.dram["attnD"], b.dbuf["attnD"] = b.dram["attn_dbg"], b.dbuf["attn_dbg"]
    else:
        b.dint("attnD", [NH, 128, NTOK], BF16)
    with ExitStack() as es0:
        b.alloc_psum(es0)
        with ExitStack() as es:
            b.attn1(es)
            b.S.flush(final=attn_only)
        with ExitStack() as es:
            if not attn_only:
                L = b.tok_alloc(es, need_attn=True)
                for j in range(nt):
                    b.load_h(L, "hTin", j)
                    b.outproj(L, 1, j)
                    b.rmsnorm(L, 5)
                    b.ffn(L, 3)
                    b.rmsnorm(L, 6, out32=("outT", j))
                b.S.flush(final=True)
    return nc, b


def build_fused():
    nc = bass.Bass("TRN2", target_bir_lowering=False)
    b = Builder(nc, n_dma_sems=30, n_cc_sems=2)
    b.fused = True
    b.din("xT", [DC, 128, NTOK])
    b.din("gains", [128, 7, DC])
    for f in range(4):
        decl_ffn_w(b, f)
    for l in range(2):
        decl_qkv_w(b, l)
        b.din("wo%d" % l, [DC, 128, NH, 128])
    b.din("cstb", [128, CSTW])
    b.din("rel_bias", [32, 16])
    b.din("Mdil", [32, XL])
    b.din("Mmoba", [32, XL])
    b.din("pastneg", [128, 256])
    b.din("notown", [128, 256])
    b.din("sbmask", [128, 8, TS])
    b.dout("outT", [DC, 128, NTOK])
    b.dint("hT", [DC, 128, NTOK])
    b.dint("qT", [NH, 128, NTOK], BF16)
    for j in range(NT):
        b.dint("kTl%d" % j, [NH * HD, TS], BF16)
        b.dint("kTg%d" % j, [2 * NH * HD, TS], BF16)
        b.dint("vl%d" % j, [TS, D], BF16)
        b.dint("vg%d" % j, [2 * TS, D], BF16)
    b.dint("kml", [8 * 128, 8])
    b.dint("kmg", [2 * 8 * 128, 8])
    b.dint("flat", [16, XL])
    b.dint("attnD", [NH, 128, NTOK], BF16)
    with ExitStack() as es0:
        b.alloc_psum(es0)
        with ExitStack() as es:
            L = b.tok_alloc(es)
            km = b.sb(es, "km", [128, 8, 8], F32, dma=True)
            nxt = b.load_h(L, "xT", 0)
            for j in range(NT):
                L["h"] = nxt
                if j + 1 < NT:
                    nxt = b.load_h(L, "xT", j + 1)
                b.rmsnorm(L, 0)
                b.ffn(L, 0)
                b.rmsnorm(L, 1)
                b.qkv(L, 0, j, km=km)
                b.store_h(L, "hT", j)
                b.exchange_tile(j)
            b.ts("dve", km.t[:], km.b, km.t[:], km.b, 1.0 / 256.0, None, ALU.mult)
            b.store("sp", b.dram["kml"].rearrange("(h p) n -> p h n", p=128), "kml", km.t[:], km)
            b.allgather("kml", "kmg")
            b.S.flush()
        with ExitStack() as es:
            b.attn0(es)
            b.S.flush()
        with ExitStack() as es:
            L = b.tok_alloc(es, need_attn=True)
            nxt = b.load_h(L, "hT", 0)
            for j in range(NT):
                L["h"] = nxt
                if j + 1 < NT:
                    nxt = b.load_h(L, "hT", j + 1)
                b.outproj(L, 0, j)
                b.rmsnorm(L, 2)
                b.ffn(L, 1)
                b.rmsnorm(L, 3)
                b.ffn(L, 2)
                b.rmsnorm(L, 4)
                b.qkv(L, 1, j)
                b.store_h(L, "hT", j)
                b.exchange_tile(j)
            b.S.flush()
        with ExitStack() as es:
            b.attn1(es)
            b.S.flush()
        with ExitStack() as es:
            L = b.tok_alloc(es, need_attn=True)
            nxt = b.load_h(L, "hT", 0)
            for j in range(NT):
                L["h"] = nxt
                if j + 1 < NT:
                    nxt = b.load_h(L, "hT", j + 1)
                b.outproj(L, 1, j)
                b.rmsnorm(L, 5)
                b.ffn(L, 3)
                b.rmsnorm(L, 6, out32=("outT", j))
            b.S.flush(final=True)
    return nc, b


def rel_bucket_np(d):
    d = np.maximum(d, 0)
    df = np.maximum(d, 1).astype(np.float32)
    large = 16 + (np.log(df / np.float32(16)) / np.float32(math.log(2048 / 16)) * np.float32(16)).astype(np.int32)
    large = np.minimum(large, 31)
    return np.where(d < 16, d, large)


def pos_tables(r):
    x = np.arange(XL)
    d = x - 1023 + r * 512
    bk = rel_bucket_np(d)
    mult = ((d <= 128).astype(np.float32) + ((d % 4 == 0) & (d <= 512)) + ((d % 16 == 0) & (d <= 2048)))
    valid = d >= 0
    Mdil = np.zeros((32, XL), np.float32)
    Mmoba = np.zeros((32, XL), np.float32)
    Mdil[bk[valid], x[valid]] = mult[valid]
    Mmoba[bk[valid], x[valid]] = 1.0
    pastneg = np.zeros((128, 4, 4, 16), np.float32)
    notown = np.ones((128, 4, 4, 16), np.float32)
    p = np.arange(128)
    for j in range(4):
        for sub in range(4):
            pos = (2 * j + r) * 512 + sub * 128 + p
            own = pos // 256
            n = np.arange(16)[None, :]
            pastneg[:, j, sub, :] = np.where(n < own[:, None], 0.0, -1e30)
            notown[:, j, sub, :] = np.where(n == own[:, None], 0.0, 1.0)
    s = np.arange(128)[:, None]
    t = np.arange(512)[None, :]
    sbmask = np.zeros((128, 8, 512), np.float32)
    tt_ = np.arange(512)
    for m in range(8):
        c = r * 512 - m * 128
        ks = tt_ + c
        ok = ks <= 127
        sbmask[np.maximum(ks[ok], 0), m, tt_[ok]] = -BIG
    return dict(Mdil=Mdil, Mmoba=Mmoba, pastneg=pastneg.reshape(128, 256),
                notown=notown.reshape(128, 256), sbmask=sbmask)


def const_tables():
    cst = np.zeros((128, CSTW), np.float32)
    cst[:, 0:128] = 1.0
    k = np.arange(128)[:, None]
    m = np.arange(128)[None, :]
    cst[:, 128:256] = (k >= m)
    sel = np.zeros((16, 16, 128), np.float32)
    for n in range(16):
        sel[n, n, :] = -BIG
    cst[0:16, 256:256 + 2048] = sel.reshape(16, 16 * 128)
    cst[:, 256 + 2048:] = (k <= m)
    return cst


def lay_unit(W, ncol):
    K, N = W.shape
    return np.ascontiguousarray(W.reshape(K // 128, 128, N // ncol, ncol).transpose(2, 1, 0, 3))


def prep_weights(inp):
    w = {}
    for i in range(2):
        for k in range(2):
            f = 2 * i + k
            w["wg%d" % f] = lay_unit(inp["ffn_w_gate"][i, k], 128)
            w["wu%d" % f] = lay_unit(inp["ffn_w_up"][i, k], 128)
            w["wd%d" % f] = lay_unit(inp["ffn_w_down"][i, k], 128)
    for l, (wq, wo) in enumerate([(inp["w_qkv_even"][0], inp["w_out_even"][0]),
                                  (inp["w_qkv_odd"][0], inp["w_out_odd"][0])]):
        w["wqk%d" % l] = lay_unit(wq[:, :2 * D], 128)
        w["wv%d" % l] = lay_unit(wq[:, 2 * D:], 256)
        w["wo%d" % l] = lay_unit(wo, 128)
    g = np.concatenate([inp["ln_gains"].reshape(6, D), inp["final_gain"][None]], 0)
    w["gains"] = np.ascontiguousarray(g.reshape(7, DC, 128).transpose(2, 0, 1))
    return w


def tok_index(r):
    return np.concatenate([np.arange((2 * j + r) * 512, (2 * j + r + 1) * 512) for j in range(4)])


_CACHE = {}


def _prog(name, fn):
    if name not in _CACHE:
        _CACHE[name] = fn()[0]
    return _CACHE[name]


def kernel(x, ln_gains, ffn_w_gate, ffn_w_up, ffn_w_down, w_qkv_even, w_out_even,
           w_qkv_odd, w_out_odd, rel_bias, final_gain):
    inp = dict(x=x, ln_gains=ln_gains, ffn_w_gate=ffn_w_gate, ffn_w_up=ffn_w_up,
               ffn_w_down=ffn_w_down, w_qkv_even=w_qkv_even, w_out_even=w_out_even,
               w_qkv_odd=w_qkv_odd, w_out_odd=w_out_odd, rel_bias=rel_bias, final_gain=final_gain)
    inp = {k: np.asarray(v, np.float32) for k, v in inp.items()}
    W = prep_weights(inp)
    cores = list(range(8))
    cst = const_tables()
    ptab = [pos_tables(r) for r in range(2)]

    def pick(names):
        return {n: W[n] for n in names}

    maps = []
    for c in cores:
        bb, r = c // 2, c % 2
        xs = inp["x"][bb][tok_index(r)]
        m = dict(W)
        m.update(xT=np.ascontiguousarray(xs.T.reshape(DC, 128, NTOK)), cstb=cst, rel_bias=inp["rel_bias"],
                 Mdil=ptab[r]["Mdil"], Mmoba=ptab[r]["Mmoba"], pastneg=ptab[r]["pastneg"],
                 notown=ptab[r]["notown"], sbmask=ptab[r]["sbmask"])
        maps.append(m)
    res = run_bass_kernel_spmd(_prog("fused", build_fused), maps, core_ids=cores).results
    out = np.empty((4, SEQ, D), np.float32)
    for c in cores:
        bb, r = c // 2, c % 2
        oT = np.asarray(res[c]["outT"]).reshape(D, NTOK)
        out[bb][tok_index(r)] = oT.T
    return out
```

```python
import math
from contextlib import ExitStack

import numpy as np
import ml_dtypes
import concourse.bass as bass
import concourse.mybir as mybir
from concourse.bass_utils import run_bass_kernel_spmd

F32 = mybir.dt.float32
BF16 = mybir.dt.bfloat16
ALU = mybir.AluOpType
AF = mybir.ActivationFunctionType
AX = mybir.AxisListType

D = 2048
DC = 16
DFF = 5632
FC = 44
SEQ = 4096
NTOK = 2048
NT = 4
TS = 512
HD = 128
NH = 16
SCALE = HD ** -0.5
EPS = 1e-6
XL = 3584
GW = 3456
BIG = 30000.0
CSTW = 256 + 16 * 128 + 128


class Dom:
    def __init__(self, name, sem, step):
        self.name, self.sem, self.step = name, sem, step
        self.total = 0
        self.last_op = None
        self.pending = []


class Buf:
    __slots__ = ("name", "last_w", "readers", "dma", "multi", "writers")

    def __init__(self, name, dma=None, multi=False):
        self.name, self.last_w, self.readers, self.dma = name, None, {}, dma
        self.multi, self.writers = multi, {}


class Op:
    __slots__ = ("eng", "dom", "emit", "deps", "signal", "sigval", "seq", "emitted", "next_sig", "tag")

    def __init__(self, eng, dom, emit, seq):
        self.eng, self.dom, self.emit, self.seq = eng, dom, emit, seq
        self.deps, self.signal, self.sigval = [], False, None
        self.emitted, self.next_sig = False, None


ENGS = ("pe", "act", "dve", "pool", "sp")


class Sched:
    def __init__(self, nc, n_dma_sems=96, n_cc_sems=0):
        self.nc = nc
        self.cdom = {e: Dom(e, nc.alloc_semaphore("sem_" + e), 1)
                     for e in ("pe", "act", "dve", "pool")}
        self.dma_pool = [Dom("dma%d" % i, nc.alloc_semaphore("semd%d" % i), 16)
                         for i in range(n_dma_sems)]
        n_sw = 9
        self.sw_pool, self.hw_pool = self.dma_pool[:n_sw], self.dma_pool[n_sw:]
        self.dma_next = {"sw": 0, "hw": 0}
        self.ops = {e: [] for e in ENGS}
        self.waited = {e: {} for e in ENGS}
        self.barrier_deps = {e: [] for e in ENGS}
        self.cc_doms = [Dom("cc%d" % i, nc.alloc_semaphore("semcc%d" % i), 1) for i in range(n_cc_sems)]
        self.cc_next = 0
        self.all_doms = list(self.cdom.values()) + self.dma_pool + self.cc_doms
        self.seq = 0
        self.n_inst = 0

    tag = ""

    def buf(self, name, dma=False, multi=False):
        d = None
        if dma:
            kind = "sw" if dma == "sw" else "hw"
            pool = self.sw_pool if kind == "sw" else self.hw_pool
            d = pool[self.dma_next[kind] % len(pool)]
            self.dma_next[kind] += 1
        return Buf(name, d, multi)

    @staticmethod
    def _res(o):
        if o.emitted and not o.signal:
            o = o.next_sig
        return o

    def _adddep(self, deps, o):
        o = self._res(o)
        cur = deps.get(o.dom)
        if cur is None or cur.seq < o.seq:
            deps[o.dom] = o

    def _record(self, eng, dom, emit, reads, writes, is_dma):
        self.seq += 1
        op = Op(eng, dom, emit, self.seq)
        op.tag = self.tag
        deps = {}
        for b in reads:
            if b.multi:
                for o in b.writers.values():
                    self._adddep(deps, o)
            elif b.last_w is not None:
                self._adddep(deps, b.last_w)
        for b in writes:
            if not b.multi and b.last_w is not None and b.last_w.dom is not dom:
                self._adddep(deps, b.last_w)
            for d, o in b.readers.items():
                if d is not dom:
                    self._adddep(deps, o)
        if is_dma:
            op.signal = True
            if dom.last_op is not None:
                self._adddep(deps, dom.last_op)
        for o in self.barrier_deps[eng]:
            self._adddep(deps, o)
        self.barrier_deps[eng] = []
        for o in deps.values():
            o.signal = True
        op.deps = list(deps.values())
        for b in reads:
            b.readers[dom] = op
        for b in writes:
            if b.multi:
                if b.readers:
                    b.writers = {}
                b.writers[dom] = op
            b.last_w = op
            b.readers = {}
        self.ops[eng].append(op)
        dom.last_op = op
        dom.pending.append(op)
        return op

    def op(self, eng, emit, reads=(), writes=()):
        return self._record(eng, self.cdom[eng], emit, reads, writes, False)

    def dma(self, queue, emit, buf, reads=(), writes=()):
        return self._record(queue, buf.dma, emit, reads, writes, True)

    def flush(self, final=False):
        nc = self.nc
        lasts = []
        for d in self.all_doms:
            if d.pending:
                d.pending[-1].signal = True
                nxt = None
                for o in reversed(d.pending):
                    if o.signal:
                        nxt = o
                    o.next_sig = nxt
                for o in d.pending:
                    if o.signal:
                        d.total += d.step
                        o.sigval = d.total
                d.pending = []
            if d.last_op is not None:
                lasts.append(d.last_op)
        streams = {e: self.ops[e] for e in ENGS}
        self.ops = {e: [] for e in ENGS}
        sched = self

        def run(ename, eng):
            w = sched.waited[ename]
            for o in streams[ename]:
                for dep in o.deps:
                    dep = sched._res(dep)
                    if w.get(dep.dom, 0) < dep.sigval:
                        eng.wait_ge(dep.dom.sem, dep.sigval)
                        w[dep.dom] = dep.sigval
                inst = o.emit(eng)
                sched.n_inst += 1
                if o.signal:
                    inst.then_inc(o.dom.sem, o.dom.step)
                o.emitted = True
            if final and ename == "sp":
                for d in sched.all_doms:
                    if d.total > 0 and w.get(d, 0) < d.total:
                        eng.wait_ge(d.sem, d.total)
                        w[d] = d.total

        with nc.Block() as block:
            @block.tensor
            def _(e):
                run("pe", e)

            @block.scalar
            def _(e):
                run("act", e)

            @block.vector
            def _(e):
                run("dve", e)

            @block.gpsimd
            def _(e):
                run("pool", e)

            @block.sync
            def _(e):
                run("sp", e)
        for e in ENGS:
            self.barrier_deps[e] = list(lasts)


def swpipe(n, stages):
    hi = max(o for o, _ in stages)
    lo = min(o for o, _ in stages)
    for i in range(-hi, n - lo):
        for off, fn in stages:
            k = i + off
            if 0 <= k < n:
                Sched.tag = "%s(%d)" % (fn.__name__, k)
                fn(k)


class T:
    __slots__ = ("t", "b")

    def __init__(self, t, b):
        self.t, self.b = t, b


class TS4:
    __slots__ = ("t", "bs")

    def __init__(self, t, bs):
        self.t, self.bs = t, bs


class Ring:
    def __init__(self, items):
        self.items, self.i = items, 0

    def next(self):
        x = self.items[self.i % len(self.items)]
        self.i += 1
        return x


class Builder:
    def __init__(self, nc, n_dma_sems=96, n_cc_sems=0):
        self.nc = nc
        self.S = Sched(nc, n_dma_sems, n_cc_sems)
        self.dram = {}
        self.dbuf = {}
        self.psum = None
        self.nuniq = 0

    def din(self, name, shape, dt=F32):
        self.dram[name] = self.nc.dram_tensor(name, list(shape), dt, kind="ExternalInput").ap()
        self.dbuf[name] = self.S.buf("d_" + name, multi=True)
        return self.dram[name]

    def dout(self, name, shape, dt=F32):
        self.dram[name] = self.nc.dram_tensor(name, list(shape), dt, kind="ExternalOutput").ap()
        self.dbuf[name] = self.S.buf("d_" + name, multi=True)
        return self.dram[name]

    def dint(self, name, shape, dt=F32):
        self.dram[name] = self.nc.dram_tensor(name, list(shape), dt, kind="Internal").ap()
        self.dbuf[name] = self.S.buf("d_" + name, multi=True)
        return self.dram[name]

    def sb(self, es, name, shape, dt, dma=False):
        self.nuniq += 1
        t = es.enter_context(self.nc.sbuf_tensor("%s_%d" % (name, self.nuniq), list(shape), dt))
        return T(t, self.S.buf(name, dma=dma))

    def alloc_psum(self, es):
        self.psum = []
        for i in range(8):
            t = es.enter_context(self.nc.psum_tensor("ps%d" % i, [128, 512], F32))
            self.psum.append(T(t, self.S.buf("ps%d" % i)))

    def mm(self, out, outb, lhsT, lb, rhs, rb, start, stop):
        self.S.op("pe", lambda e: e.matmul(out, lhsT=lhsT, rhs=rhs, start=start, stop=stop),
                  reads=[lb, rb], writes=[outb])

    def act(self, out, outb, in_, inb, func, scale=1.0, bias=None, extra_reads=()):
        if bias is None:
            self.S.op("act", lambda e: e.activation(out=out, in_=in_, func=func, scale=scale),
                      reads=[inb, *extra_reads], writes=[outb])
        else:
            self.S.op("act", lambda e: e.activation(out=out, in_=in_, func=func, scale=scale, bias=bias),
                      reads=[inb, *extra_reads], writes=[outb])

    def tt(self, eng, out, outb, in0, b0, in1, b1, op):
        self.S.op(eng, lambda e: e.tensor_tensor(out=out, in0=in0, in1=in1, op=op),
                  reads=[b0, b1], writes=[outb])

    def stt(self, eng, out, outb, in0, b0, scalar, sb_, in1, b1, op0, op1):
        rd = [b0, b1] + ([sb_] if sb_ is not None else [])
        self.S.op(eng, lambda e: e.scalar_tensor_tensor(out=out, in0=in0, scalar=scalar, in1=in1,
                                                        op0=op0, op1=op1),
                  reads=rd, writes=[outb])

    def ts(self, eng, out, outb, in0, b0, s1, s2, op0, op1=None, sbufs=()):
        if op1 is None:
            self.S.op(eng, lambda e: e.tensor_scalar(out=out, in0=in0, scalar1=s1, scalar2=None, op0=op0),
                      reads=[b0, *sbufs], writes=[outb])
        else:
            self.S.op(eng, lambda e: e.tensor_scalar(out=out, in0=in0, scalar1=s1, scalar2=s2,
                                                     op0=op0, op1=op1),
                      reads=[b0, *sbufs], writes=[outb])

    def copy(self, eng, out, outb, in_, inb):
        if eng == "act":
            self.act(out, outb, in_, inb, AF.Copy)
        else:
            self.S.op(eng, lambda e: e.tensor_copy(out=out, in_=in_), reads=[inb], writes=[outb])

    def memset(self, eng, out, outb, val):
        self.S.op(eng, lambda e: e.memset(out, val), writes=[outb])

    def load(self, q, dst, dstT, src, srcname):
        self.S.dma(q, lambda e: e.dma_start(out=dst, in_=src), dstT.b,
                   reads=[self.dbuf[srcname]], writes=[dstT.b])

    def store(self, q, dst, dstname, src, srcT):
        self.S.dma(q, lambda e: e.dma_start(out=dst, in_=src), srcT.b,
                   reads=[srcT.b], writes=[self.dbuf[dstname]])

    def tok_alloc(self, es, need_attn=False):
        b = self
        L = {}
        L["hring"] = Ring([b.sb(es, "hsb%d" % i, [128, DC, TS], F32, dma=True) for i in range(2)])
        L["h"] = None
        L["u"] = b.sb(es, "usb", [128, DC, TS], BF16)
        L["ub"] = [b.S.buf("u%d" % i) for i in range(DC)]
        L["a"] = b.sb(es, "asb", [128, FC, TS], BF16)
        L["sq"] = Ring([b.sb(es, "sq%d" % i, [128, TS], BF16) for i in range(4)])
        L["rstd"] = b.sb(es, "rstd", [128, TS], F32)
        L["sg"] = Ring([b.sb(es, "sg%d" % i, [128, TS], F32) for i in range(2)])
        L["wa"] = Ring([b.sb(es, "wa%d" % i, [128, DC, 128], BF16, dma="sw") for i in range(4)])
        L["wd"] = Ring([b.sb(es, "wd%d" % i, [128, FC * 128], BF16, dma="sw") for i in range(3)])
        L["gains"] = b.sb(es, "gains", [128, 7 * DC], F32, dma=True)
        L["onesD"] = b.sb(es, "onesD", [128, 128], BF16)
        L["ev"] = Ring([b.sb(es, "ev%d" % i, [128, TS], BF16, dma=True) for i in range(4)])
        b.memset("pool", L["onesD"].t[:], L["onesD"].b, 1.0 / D)
        b.load("sp", L["gains"].t[:], L["gains"], b.dram["gains"].rearrange("p k c -> p (k c)"), "gains")
        L["pg"] = Ring([b.psum[0], b.psum[1]])
        L["pu"] = Ring([b.psum[2], b.psum[3]])
        L["pd"] = Ring([b.psum[4], b.psum[5]])
        L["pn"] = b.psum[6]
        if need_attn:
            L["attn"] = b.sb(es, "attn_t", [128, NH, TS], BF16, dma=True)
        L["o32"] = Ring([b.sb(es, "o32_%d" % i, [128, TS], F32, dma=True) for i in range(2)])
        return L

    def load_h(self, L, src, j):
        hb = L["hring"].next()
        d = self.dram[src].rearrange("c p t -> p c t")[:, :, j * TS:(j + 1) * TS]
        for c0 in range(0, DC, 4):
            self.load("sp", hb.t[:, c0:c0 + 4, :], hb, d[:, c0:c0 + 4, :], src)
        return hb

    def store_h(self, L, dst, j):
        hb = L["h"]
        d = self.dram[dst].rearrange("c p t -> p c t")[:, :, j * TS:(j + 1) * TS]
        for c0 in range(0, DC, 4):
            self.store("sp", d[:, c0:c0 + 4, :], dst, hb.t[:, c0:c0 + 4, :], hb)

    def rmsnorm(self, L, gidx, out32=None):
        b = self
        h, pn = L["h"], L["pn"]
        for dc in range(DC):
            sq = L["sq"].next()
            if dc % 2 == 0:
                b.act(sq.t[:], sq.b, h.t[:, dc, :], h.b, AF.Square)
            else:
                b.tt("dve", sq.t[:], sq.b, h.t[:, dc, :], h.b, h.t[:, dc, :], h.b, ALU.mult)
            b.mm(pn.t[:], pn.b, L["onesD"].t[:], L["onesD"].b, sq.t[:], sq.b, dc == 0, dc == DC - 1)
        r = L["rstd"]
        b.act(r.t[:], r.b, pn.t[:], pn.b, AF.Sqrt, bias=EPS)
        b.S.op("dve", lambda e: e.reciprocal(out=r.t[:], in_=r.t[:]), reads=[r.b], writes=[r.b])
        g = L["gains"]
        for dc in range(DC):
            col = gidx * DC + dc
            if out32 is None:
                dst = L["u"]
                b.stt("dve", dst.t[:, dc, :], L["ub"][dc], h.t[:, dc, :], h.b, g.t[:, col:col + 1], g.b,
                      r.t[:], r.b, ALU.mult, ALU.mult)
            else:
                name, j = out32
                o = L["o32"].next()
                b.stt("dve", o.t[:], o.b, h.t[:, dc, :], h.b, g.t[:, col:col + 1], g.b,
                      r.t[:], r.b, ALU.mult, ALU.mult)
                b.store("sp", b.dram[name][dc][:, j * TS:(j + 1) * TS], name, o.t[:], o)

    def load_wa(self, L, name, idx):
        w = L["wa"].next()
        self.load("pool", w.t[:], w, self.dram[name][idx], name)
        return w

    def ffn(self, L, f):
        b = self
        u, a, h = L["u"], L["a"], L["h"]
        for c in range(FC):
            wg = b.load_wa(L, "wg%d" % f, c)
            wu = b.load_wa(L, "wu%d" % f, c)
            pg, pu = L["pg"].next(), L["pu"].next()
            for dc in range(DC):
                b.mm(pg.t[:], pg.b, wg.t[:, dc, :], wg.b, u.t[:, dc, :], L["ub"][dc], dc == 0, dc == DC - 1)
            for dc in range(DC):
                b.mm(pu.t[:], pu.b, wu.t[:, dc, :], wu.b, u.t[:, dc, :], L["ub"][dc], dc == 0, dc == DC - 1)
            sg = L["sg"].next()
            b.act(sg.t[:], sg.b, pg.t[:], pg.b, AF.Silu)
            b.tt("dve", a.t[:, c, :], a.b, sg.t[:], sg.b, pu.t[:], pu.b, ALU.mult)
        for dmc in range(DC):
            w = L["wd"].next()
            wv = w.t[:, 0:FC * 128].rearrange("p (c m) -> p c m", m=128)
            b.load("pool", wv, w, b.dram["wd%d" % f][dmc], "wd%d" % f)
            if dmc in (4, 11) and L.get("pending_cc"):
                src, dst = L["pending_cc"].pop(0)
                b.allgather(src, dst)
            pd = L["pd"].next()
            for c in range(FC):
                b.mm(pd.t[:], pd.b, wv[:, c, :], w.b, a.t[:, c, :], a.b, c == 0, c == FC - 1)
            b.stt("dve", h.t[:, dmc, :], h.b, pd.t[:], pd.b, 0.5, None, h.t[:, dmc, :], h.b,
                  ALU.mult, ALU.add)

    def qkv(self, L, l, j, km=None):
        import os
        b = self
        u = L["u"]
        dbg = os.environ.get("QKV_DBG", "")
        if "nokm" in dbg:
            km = None
        pq = Ring([b.psum[0], b.psum[1], b.psum[2], b.psum[3]])
        for un in range(0 if "noqk" not in dbg else 32, 32):
            w = b.load_wa(L, "wqk%d" % l, un)
            p = pq.next()
            for dc in range(DC):
                b.mm(p.t[:], p.b, w.t[:, dc, :], w.b, u.t[:, dc, :], L["ub"][dc], dc == 0, dc == DC - 1)
            ev = L["ev"].next()
            b.copy("dve", ev.t[:], ev.b, p.t[:], p.b)
            if un < 16:
                b.store("sp", b.dram["qT"][un][:, j * TS:(j + 1) * TS], "qT", ev.t[:], ev)
            else:
                nm, ap = b.k_dst(un - 16, j)
                b.store("sp", ap, nm, ev.t[:], ev)
            if km is not None and un >= 24:
                hm = un - 24
                b.S.op("dve", lambda e, hm=hm, p=p: e.reduce_sum(
                    out=km.t[:, hm, 2 * j:2 * j + 2],
                    in_=p.t[:].rearrange("p (b k) -> p b k", b=2), axis=AX.X),
                    reads=[p.b], writes=[km.b])
        pv = Ring([b.psum[4], b.psum[5]])
        for g in range(4):
            ws = []
            for hf in range(2):
                w = L["wd"].next()
                wv = w.t[:, 0:DC * 256].rearrange("p (c n) -> p c n", n=256)
                b.load("pool", wv, w, b.dram["wv%d" % l][2 * g + hf], "wv%d" % l)
                ws.append((w, wv))
            for sub in range(4):
                p = pv.next()
                for hf in range(2):
                    w, wv = ws[hf]
                    for dc in range(DC):
                        b.mm(p.t[:, hf * 256:(hf + 1) * 256], p.b, u.t[:, dc, sub * 128:(sub + 1) * 128],
                             L["ub"][dc], wv[:, dc, :], w.b, dc == 0, dc == DC - 1)
                ev = L["ev"].next()
                b.copy("dve", ev.t[:], ev.b, p.t[:], p.b)
                nm, ap = b.v_dst(j, sub, g)
                b.store("sp", ap, nm, ev.t[:], ev)

    def outproj(self, L, l, j):
        b = self
        h = L["h"]
        attn = L["attn"]
        asrc = b.dram["attnD"].rearrange("h p t -> p h t")[:, :, j * TS:(j + 1) * TS]
        for c0 in range(0, NH, 4):
            b.load("sp", attn.t[:, c0:c0 + 4, :], attn, asrc[:, c0:c0 + 4, :], "attnD")
        for dmc in range(DC):
            w = b.load_wa(L, "wo%d" % l, dmc)
            pd = L["pd"].next()
            for hc in range(NH):
                b.mm(pd.t[:], pd.b, w.t[:, hc, :], w.b, attn.t[:, hc, :], attn.b,
                     hc == 0, hc == NH - 1)
            b.tt("dve", h.t[:, dmc, :], h.b, pd.t[:], pd.b, h.t[:, dmc, :], h.b, ALU.add)

    def attn_alloc(self, es):
        b = self
        A = {}
        def sb4(name, shape):
            t = b.sb(es, name, shape, BF16)
            return TS4(t.t, [b.S.buf("%s_%d" % (name, a), dma=True) for a in range(4)])
        A["kT"] = Ring([sb4("kT%d" % i, [128, SEQ]) for i in range(2)])
        A["v"] = Ring([sb4("v%d" % i, [128, 32, HD]) for i in range(2)])
        A["q"] = Ring([b.sb(es, "q%d" % i, [128, NTOK], BF16, dma=True) for i in range(2)])
        A["e"] = Ring([b.sb(es, "e%d" % i, [128, TS], F32) for i in range(6)])
        A["p"] = Ring([b.sb(es, "p%d" % i, [128, TS], BF16) for i in range(4)])
        A["ao"] = Ring([b.sb(es, "ao%d" % i, [128, TS], BF16, dma=True) for i in range(2)])
        A["cst"] = b.sb(es, "cst", [128, CSTW], BF16, dma="sw")
        b.load("pool", A["cst"].t[:], A["cst"], b.dram["cstb"], "cstb")
        return A

    def load_head(self, A, h):
        b = self
        kT, v, q = A["kT"].next(), A["v"].next(), A["q"].next()
        def ld(dst, buf, src, nm):
            b.S.dma("sp", lambda e: e.dma_start(out=dst, in_=src), buf, reads=[b.dbuf[nm]], writes=[buf])

        if b.fused:
            for a in range(4):
                for r in range(2):
                    Tt = 2 * a + r
                    nm = "kTg%d" % a
                    ld(kT.t[:, Tt * TS:(Tt + 1) * TS], kT.bs[a],
                       b.dram[nm][r * 2048 + h * HD:r * 2048 + (h + 1) * HD, :], nm)
            for a in range(4):
                for r in range(2):
                    Tt = 2 * a + r
                    nm = "vg%d" % a
                    srcv = b.dram[nm][r * TS:(r + 1) * TS, h * HD:(h + 1) * HD].rearrange("(q p) d -> p q d", p=128)
                    ld(v.t[:, Tt * 4:(Tt + 1) * 4, :], v.bs[a], srcv, nm)
        else:
            for r in range(2):
                for a in range(4):
                    Tt = 2 * a + r
                    src = b.dram["kTg"][r][h][:, a * TS:(a + 1) * TS]
                    ld(kT.t[:, Tt * TS:(Tt + 1) * TS], kT.bs[a], src, "kTg")
                    srcv = b.dram["vg"][r][a * TS:(a + 1) * TS, h * HD:(h + 1) * HD].rearrange(
                        "(q p) d -> p q d", p=128)
                    ld(v.t[:, Tt * 4:(Tt + 1) * 4, :], v.bs[a], srcv, "vg")
        b.load("sp", q.t[:], q, b.dram["qT"][h], "qT")
        return kT, v, q

    fused = False

    def k_dst(self, h, j):
        if self.fused:
            nm = "kTl%d" % j
            return nm, self.dram[nm][h * HD:(h + 1) * HD, :]
        return "kTl", self.dram["kTl"][h][:, j * TS:(j + 1) * TS]

    def v_dst(self, j, sub, g):
        if self.fused:
            nm = "vl%d" % j
            return nm, self.dram[nm][sub * 128:(sub + 1) * 128, g * 512:(g + 1) * 512]
        r0 = j * TS + sub * 128
        return "vl", self.dram["vl"][r0:r0 + 128, g * 512:(g + 1) * 512]

    def km_src(self, r, hm):
        if self.fused:
            return self.dram["kmg"][r * 1024 + hm * 128:r * 1024 + (hm + 1) * 128, :].rearrange("p (a c) -> p a c", c=2)
        return self.dram["kmg"][r][hm].rearrange("p (a c) -> p a c", c=2)

    def allgather(self, src, dst):
        S = self.S
        d = S.cc_doms[S.cc_next % len(S.cc_doms)]
        S.cc_next += 1
        cb = Buf("cc", d)
        si, so = self.dram[src].opt(), self.dram[dst].opt()
        S.dma("pool", lambda e: e.collective_compute("AllGather", ALU.bypass,
                                                     replica_groups=[[0, 1], [2, 3], [4, 5], [6, 7]],
                                                     ins=[si], outs=[so]),
              cb, reads=[self.dbuf[src]], writes=[self.dbuf[dst]])

    def exchange_tile(self, j, L=None, defer=False):
        pairs = [("kTl%d" % j, "kTg%d" % j), ("vl%d" % j, "vg%d" % j)]
        if defer:
            L.setdefault("pending_cc", []).extend(pairs)
        else:
            for src, dst in pairs:
                self.allgather(src, dst)

    def flush_cc(self, L):
        for src, dst in L.get("pending_cc", []):
            self.allgather(src, dst)
        L["pending_cc"] = []

    def attn_out(self, A, src, srcb, h, j, rden=None):
        b = self
        ao = A["ao"].next()
        if rden is None:
            b.copy("dve", ao.t[:], ao.b, src, srcb)
        else:
            b.tt("dve", ao.t[:], ao.b, src, srcb, rden.t[:], rden.b, ALU.mult)
        b.store("sp", b.dram["attnD"][h][:, j * TS:(j + 1) * TS], "attnD", ao.t[:], ao)

    def attn0(self, es):
        b = self
        A = b.attn_alloc(es)
        ones = A["cst"].t[:, 0:128]
        cb = A["cst"].b
        selrow = A["cst"].t[0:16, 256:256 + 16 * 128].rearrange("p (n m) -> p n m", m=128)
        rel = b.sb(es, "rel", [32, 16], F32, dma=True)
        erel = b.sb(es, "erel", [32, 16], F32)
        Mt = b.sb(es, "Mt", [32, 2, XL], F32, dma=True)
        fl = b.sb(es, "fl", [8, 2, XL], F32, dma=True)
        tabfar = b.sb(es, "tabfar", [128, 8], F32, dma=True)
        Jm = b.sb(es, "Jm", [128, 128], F32)
        b.load("sp", rel.t[:], rel, b.dram["rel_bias"], "rel_bias")
        b.load("sp", Mt.t[:, 0, :], Mt, b.dram["Mdil"], "Mdil")
        b.load("sp", Mt.t[:, 1, :], Mt, b.dram["Mmoba"], "Mmoba")
        b.load("sp", tabfar.t[:], tabfar, b.dram["rel_bias"][31:32, 8:16].partition_broadcast(128), "rel_bias")
        b.act(erel.t[:], erel.b, rel.t[:], rel.b, AF.Exp)
        b.memset("pool", Jm.t[:], Jm.b, 0.0)
        b.S.op("pool", lambda e: e.affine_select(out=Jm.t[:], in_=Jm.t[:], pattern=[[1, 128]],
                                                 compare_op=ALU.not_equal, fill=1.0, base=-127,
                                                 channel_multiplier=1),
               reads=[Jm.b], writes=[Jm.b])
        pm = b.psum[6]
        for k in range(2):
            for c in range(XL // 512):
                b.mm(pm.t[0:8, :], pm.b, erel.t[:, 8 * k:8 * k + 8], erel.b,
                     Mt.t[:, k, c * 512:(c + 1) * 512], Mt.b, True, True)
                b.copy("dve", fl.t[:, k, c * 512:(c + 1) * 512], fl.b, pm.t[0:8, :], pm.b)
            b.store("sp", b.dram["flat"][8 * k:8 * k + 8, :], "flat", fl.t[:, k, :], fl)
        pastneg = b.sb(es, "pastneg", [128, 256], F32, dma=True)
        notown = b.sb(es, "notown", [128, 256], F32, dma=True)
        b.load("sp", pastneg.t[:], pastneg, b.dram["pastneg"], "pastneg")
        b.load("sp", notown.t[:], notown, b.dram["notown"], "notown")
        ident = b.sb(es, "ident", [128, 128], F32)
        b.memset("pool", ident.t[:], ident.b, 0.0)
        b.S.op("pool", lambda e: e.affine_select(out=ident.t[:], in_=ident.t[:], pattern=[[-1, 128]],
                                                 compare_op=ALU.not_equal, fill=1.0, base=0,
                                                 channel_multiplier=1),
               reads=[ident.b], writes=[ident.b])
        gsp = b.sb(es, "gsp", [128, GW], F32, dma=True)
        gs_ring = Ring([b.sb(es, "gs%d" % i, [128, GW], F32) for i in range(2)])
        kmf = b.sb(es, "kmf", [128, 16], F32, dma=True)
        kmb = b.sb(es, "kmb", [128, 16], BF16)
        gm = b.sb(es, "gm", [128, 64], F32)
        mx8 = b.sb(es, "mx8", [128, 4, 8], F32)
        thr = b.sb(es, "thr", [128, 4], F32)
        nsel = b.sb(es, "nsel", [128, 64], F32)
        nselT = Ring([b.sb(es, "nselT%d" % i, [16, TS], BF16) for i in range(2)])
        rden = b.sb(es, "rden", [128, TS], F32)
        lden = b.sb(es, "lden", [128, TS], F32)
        pfar = Ring([b.sb(es, "pfar%d" % i, [128, TS], BF16) for i in range(6)])
        ghi = b.sb(es, "ghi", [128, GW], BF16)
        glo = b.sb(es, "glo", [128, GW], BF16)
        Jb = b.sb(es, "Jb", [128, 128], BF16)
        b.copy("pool", Jb.t[:], Jb.b, Jm.t[:], Jm.b)
        pj_ring = Ring([b.psum[6], b.psum[7]])
        ps_ring = Ring([b.psum[0], b.psum[1]])
        po_ring = Ring([b.psum[2], b.psum[3]])
        pden_ring = Ring([b.psum[4], b.psum[5]])
        mult_eng = Ring(["dve"])

        def prep_load(h):
            hd = b.load_head(A, h)
            src = bass.AP(b.dram["flat"].tensor, h * XL, [[1, 128], [1, GW]])
            b.load("sp", gsp.t[:], gsp, src, "flat")
            b.copy("pool", ghi.t[:], ghi.b, gsp.t[:], gsp.b)
            b.tt("pool", glo.t[:], glo.b, gsp.t[:], gsp.b, ghi.t[:], ghi.b, ALU.subtract)
            return hd

        def prep_table(hd):
            gs = gs_ring.next()
            for c in range((GW + 511) // 512):
                w = min(512, GW - c * 512)
                pj = pj_ring.next()
                b.mm(pj.t[:, 0:w], pj.b, Jb.t[:], Jb.b, ghi.t[:, c * 512:c * 512 + w], ghi.b, True, False)
                b.mm(pj.t[:, 0:w], pj.b, Jb.t[:], Jb.b, glo.t[:, c * 512:c * 512 + w], glo.b, False, True)
                b.copy("dve", gs.t[:, c * 512:c * 512 + w], gs.b, pj.t[:, 0:w], pj.b)
            return hd, gs

        def load_km(h):
            hm = h - 8
            for r in range(2):
                dst = kmf.t[:].rearrange("p (a r c) -> p a r c", r=2, c=2)[:, :, r, :]
                b.load("sp", dst, kmf, b.km_src(r, hm), "kmg")
            b.copy("dve", kmb.t[:], kmb.b, kmf.t[:], kmf.b)

        def gate(h, j, q):
            pg = b.psum[7]
            for sub in range(4):
                b.mm(pg.t[:, sub * 16:(sub + 1) * 16], pg.b,
                     q.t[:, j * TS + sub * 128:j * TS + (sub + 1) * 128], q.b, kmb.t[:], kmb.b, True, True)
            b.tt("dve", gm.t[:], gm.b, pg.t[:, 0:64], pg.b, pastneg.t[:, j * 64:(j + 1) * 64], pastneg.b, ALU.add)
            for sub in range(4):
                b.S.op("dve", lambda e, sub=sub: e.max(out=mx8.t[:, sub, :], in_=gm.t[:, sub * 16:(sub + 1) * 16]),
                       reads=[gm.b], writes=[mx8.b])
            b.ts("dve", thr.t[:], thr.b, mx8.t[:, :, 2], mx8.b, -1e29, None, ALU.max)
            for sub in range(4):
                b.ts("dve", nsel.t[:, sub * 16:(sub + 1) * 16], nsel.b, gm.t[:, sub * 16:(sub + 1) * 16], gm.b,
                     thr.t[:, sub:sub + 1], None, ALU.is_lt, sbufs=[thr.b])
            b.tt("dve", nsel.t[:], nsel.b, nsel.t[:], nsel.b, notown.t[:, j * 64:(j + 1) * 64], notown.b, ALU.mult)
            for sub in range(4):
                b.S.op("pe", lambda e, sub=sub: e.transpose(b.psum[6].t[0:16, sub * 128:(sub + 1) * 128],
                                                            nsel.t[:, sub * 16:(sub + 1) * 16], ident.t[:]),
                       reads=[nsel.b, ident.b], writes=[b.psum[6].b])
            nT = nselT.next()
            b.copy("dve", nT.t[:], nT.b, b.psum[6].t[0:16, :], b.psum[6].b)
            return nT

        steps = []
        for h in range(NH):
            for j in range(NT):
                kb_lo = 0 if h >= 8 else max(0, 8 * j - 16)
                kbs = list(range(kb_lo, 8 * j + 8))
                for k, kb in enumerate(kbs):
                    steps.append((h, j, k, kb, len(kbs)))
        G = len(steps)
        st = [dict() for _ in range(G)]
        heads = {0: prep_table(prep_load(0))}
        pend = {}
        tiles = {}

        def fS(g):
            h, j, k, kb, n = steps[g]
            moba = h >= 8
            if j == 0 and k == 6 and h + 1 < NH:
                pend[h + 1] = prep_load(h + 1)
            if j == 2 and k == 0 and h + 1 < NH:
                heads[h + 1] = prep_table(pend.pop(h + 1))
            (kT, v, q), gs = heads[h]
            if k == 0:
                tiles.setdefault((h, j), {})
                if g == 0 and moba:
                    load_km(h)
                    tiles[(h, j)]["nT"] = gate(h, j, q)
                h2, j2 = (h, j + 1) if j + 1 < NT else (h + 1, 0)
                if h2 < NH and h2 >= 8:
                    if j2 == 0:
                        load_km(h2)
                    tiles.setdefault((h2, j2), {})["nT"] = gate(h2, j2, heads[h2][0][2])
            ps = ps_ring.next()
            b.mm(ps.t[:], ps.b, kT.t[:, kb * 128:(kb + 1) * 128], kT.bs[kb // 8], q.t[:, j * TS:(j + 1) * TS], q.b,
                 True, not moba)
            if moba:
                nT = tiles[(h, j)]["nT"]
                b.mm(ps.t[:], ps.b, selrow[:, kb // 2, :], cb, nT.t[:], nT.b, False, True)
            st[g]["ps"] = ps

        def fE(g):
            h, j, k, kb, n = steps[g]
            moba = h >= 8
            ps = st[g].pop("ps")
            delta = (8 * j - kb) * 128
            far = moba and delta >= 1664
            st[g]["delta"] = delta
            if far:
                p = pfar.next()
                b.act(p.t[:], p.b, ps.t[:], ps.b, AF.Exp, scale=SCALE,
                      bias=tabfar.t[:, h - 8:h - 7], extra_reads=[tabfar.b])
                st[g]["p"] = p
            else:
                e_ = A["e"].next()
                b.act(e_.t[:], e_.b, ps.t[:], ps.b, AF.Exp, scale=SCALE)
                st[g]["e"] = e_

        def fP(g):
            h = steps[g][0]
            if "e" in st[g]:
                gs = heads[h][1]
                e_ = st[g].pop("e")
                p = A["p"].next()
                i0 = st[g]["delta"] + 896
                b.tt(mult_eng.next(), p.t[:], p.b, e_.t[:], e_.b, gs.t[:, i0:i0 + TS], gs.b, ALU.mult)
                st[g]["p"] = p

        def fPV(g):
            h, j, k, kb, n = steps[g]
            tl = tiles[(h, j)]
            if k == 0:
                tl["po"], tl["pden"] = po_ring.next(), pden_ring.next()
            po, pden = tl["po"], tl["pden"]
            v = heads[h][0][1]
            p = st[g].pop("p")
            b.mm(po.t[:], po.b, v.t[:, kb, :], v.bs[kb // 8], p.t[:], p.b, k == 0, k == n - 1)
            b.mm(pden.t[:], pden.b, ones, cb, p.t[:], p.b, k == 0, k == n - 1)
            if k == n - 1:
                b.act(lden.t[:], lden.b, pden.t[:], pden.b, AF.Ln)
                b.act(rden.t[:], rden.b, lden.t[:], lden.b, AF.Exp, scale=-1.0)
                b.attn_out(A, po.t[:], po.b, h, j, rden=rden)

        swpipe(G, [(3, fS), (3, fE), (1, fP), (0, fPV)])

    def attn1(self, es):
        b = self
        A = b.attn_alloc(es)
        ones = A["cst"].t[:, 0:128]
        tri = A["cst"].t[:, 128:256]
        cb = A["cst"].b
        tri2 = A["cst"].t[:, 256 + 2048:256 + 2048 + 128]
        sbm = b.sb(es, "sbm", [128, 8, TS], BF16, dma="sw")
        b.load("pool", sbm.t[:], sbm, b.dram["sbmask"], "sbmask")
        Sb_ring = Ring([b.sb(es, "Sb%d" % i, [128, TS], BF16) for i in range(5)])
        sp_ring = Ring([b.sb(es, "sp%d" % i, [128, TS], BF16) for i in range(4)])
        ew_ring = Ring([b.sb(es, "ew%d" % i, [128, TS], F32) for i in range(2)])
        pz_ring = Ring([b.psum[0], b.psum[1]])
        pw_ring = Ring([b.psum[2], b.psum[3], b.psum[4]])
        po_ring = Ring([b.psum[5], b.psum[6]])
        steps = []
        for h in range(NH):
            for j in range(NT):
                kbs = list(range(8 * j + 7, -1, -1))
                for k, kb in enumerate(kbs):
                    steps.append((h, j, k, kb, len(kbs)))
        G = len(steps)
        st = [dict() for _ in range(G)]
        heads = {0: b.load_head(A, 0)}
        pos = {}

        def fZ(g):
            h, j, k, kb, n = steps[g]
            if j == 0 and k == 6 and h + 1 < NH:
                heads[h + 1] = b.load_head(A, h + 1)
            kT, v, q = heads[h]
            pz = pz_ring.next()
            diag = kb >= 8 * j
            b.mm(pz.t[:], pz.b, kT.t[:, kb * 128:(kb + 1) * 128], kT.bs[kb // 8], q.t[:, j * TS:(j + 1) * TS], q.b,
                 True, not diag)
            if diag:
                b.mm(pz.t[:], pz.b, tri2, cb, sbm.t[:, kb - 8 * j, :], sbm.b, False, True)
            st[g]["pz"] = pz

        def fE(g):
            pz = st[g].pop("pz")
            e_ = A["e"].next()
            b.act(e_.t[:], e_.b, pz.t[:], pz.b, AF.Exp, scale=SCALE)
            st[g]["e"] = e_

        def fSP(g):
            e_ = st[g]["e"]
            sp = sp_ring.next()
            b.act(sp.t[:], sp.b, e_.t[:], e_.b, AF.Ln, bias=1.0)
            st[g]["sp"] = sp

        def fPW(g):
            k = steps[g][2]
            sp = st[g]["sp"]
            pw = pw_ring.next()
            b.mm(pw.t[:], pw.b, tri, cb, sp.t[:], sp.b, True, k == 0)
            if k > 0:
                Sp = st[g - 1]["Sb"]
                b.mm(pw.t[:], pw.b, ones, cb, Sp.t[:], Sp.b, False, True)
            st[g]["pw"] = pw

        def fS(g):
            k = steps[g][2]
            sp = st[g]["sp"]
            Sn = Sb_ring.next()
            if k == 0:
                b.copy("dve", Sn.t[:], Sn.b, sp.t[:], sp.b)
            else:
                Sp = st[g - 1]["Sb"]
                b.tt("dve", Sn.t[:], Sn.b, Sp.t[:], Sp.b, sp.t[:], sp.b, ALU.add)
            st[g]["Sb"] = Sn

        def fEW(g):
            pw = st[g].pop("pw")
            ew = ew_ring.next()
            b.act(ew.t[:], ew.b, pw.t[:], pw.b, AF.Exp, scale=-1.0)
            st[g]["ew"] = ew

        def fA(g):
            e_, ew = st[g].pop("e"), st[g].pop("ew")
            a_ = A["p"].next()
            b.tt("dve", a_.t[:], a_.b, e_.t[:], e_.b, ew.t[:], ew.b, ALU.mult)
            st[g]["a"] = a_

        def fPV(g):
            h, j, k, kb, n = steps[g]
            a_ = st[g].pop("a")
            if k == 0:
                pos[(h, j)] = po_ring.next()
            po = pos[(h, j)]
            v = heads[h][1]
            b.mm(po.t[:], po.b, v.t[:, kb, :], v.bs[kb // 8], a_.t[:], a_.b, k == 0, k == n - 1)
            if k == n - 1:
                b.attn_out(A, po.t[:], po.b, h, j)
                if g >= 8:
                    st[g - 8].clear()

        swpipe(G, [(3, fZ), (3, fE), (0, fEW), (0, fA), (3, fSP), (1, fPW), (3, fS), (-2, fPV)])


def decl_ffn_w(b, f):
    b.din("wg%d" % f, [FC, 128, DC, 128])
    b.din("wu%d" % f, [FC, 128, DC, 128])
    b.din("wd%d" % f, [DC, 128, FC, 128])


def decl_qkv_w(b, l):
    b.din("wqk%d" % l, [32, 128, DC, 128])
    b.din("wv%d" % l, [8, 128, DC, 256])


def build_L1(nt=NT, level=9):
    nc = bass.Bass("TRN2", target_bir_lowering=False)
    b = Builder(nc)
    b.din("xT", [DC, 128, NTOK])
    b.din("gains", [128, 7, DC])
    decl_ffn_w(b, 0)
    decl_qkv_w(b, 0)
    b.dout("hT", [DC, 128, NTOK])
    b.dout("qT", [NH, 128, NTOK], BF16)
    b.dout("kTl", [NH, 128, NTOK], BF16)
    b.dout("vl", [NTOK, D], BF16)
    b.dout("kml", [8, 128, 8])
    with ExitStack() as es:
        b.alloc_psum(es)
        L = b.tok_alloc(es)
        km = b.sb(es, "km", [128, 8, 8], F32, dma=True)
        for j in range(nt):
            b.load_h(L, "xT", j)
            if level >= 1:
                b.rmsnorm(L, 0)
            if level >= 2:
                b.ffn(L, 0)
            if level >= 3:
                b.rmsnorm(L, 1)
                b.qkv(L, 0, j, km=km)
            b.store_h(L, "hT", j)
        if level >= 3:
            b.ts("dve", km.t[:], km.b, km.t[:], km.b, 1.0 / 256.0, None, ALU.mult)
            b.store("sp", b.dram["kml"].rearrange("h p n -> p h n"), "kml", km.t[:], km)
        b.S.flush(final=True)
    return nc, b


def decl_attn_in(b, layer0, attn_only=False):
    if not attn_only:
        b.din("hTin", [DC, 128, NTOK])
    b.din("qT", [NH, 128, NTOK], BF16)
    b.din("kTg", [2, NH, 128, NTOK], BF16)
    b.din("vg", [2, NTOK, D], BF16)
    b.din("cstb", [128, CSTW])
    if layer0:
        b.din("kmg", [2, 8, 128, 8])
        b.din("rel_bias", [32, 16])
        b.din("Mdil", [32, XL])
        b.din("Mmoba", [32, XL])
        b.din("pastneg", [128, 256])
        b.din("notown", [128, 256])
        b.dint("flat", [16, XL])
    else:
        b.din("sbmask", [128, 8, TS])


def build_L2(nt=NT, attn_only=False):
    nc = bass.Bass("TRN2", target_bir_lowering=False)
    b = Builder(nc)
    decl_attn_in(b, True, attn_only)
    if not attn_only:
        b.din("gains", [128, 7, DC])
        b.din("wo0", [DC, 128, NH, 128])
        decl_ffn_w(b, 1)
        decl_ffn_w(b, 2)
        decl_qkv_w(b, 1)
        b.dout("hT", [DC, 128, NTOK])
        b.dout("qTo", [NH, 128, NTOK], BF16)
        b.dout("kTl", [NH, 128, NTOK], BF16)
        b.dout("vl", [NTOK, D], BF16)
    if attn_only:
        b.dout("attn_dbg", [NH, 128, NTOK], BF16)
    if attn_only:
        b.dram["attnD"], b.dbuf["attnD"] = b.dram["attn_dbg"], b.dbuf["attn_dbg"]
    else:
        b.dint("attnD", [NH, 128, NTOK], BF16)
    with ExitStack() as es0:
        b.alloc_psum(es0)
        with ExitStack() as es:
            b.attn0(es)
            b.S.flush(final=attn_only)
        with ExitStack() as es:
            if not attn_only:
                L = b.tok_alloc(es, need_attn=True)
                b.dram["qT_in"] = b.dram["qT"]
                b.dram["qT"] = b.dram["qTo"]
                b.dbuf["qT"] = b.dbuf["qTo"]
                for j in range(nt):
                    b.load_h(L, "hTin", j)
                    b.outproj(L, 0, j)
                    b.rmsnorm(L, 2)
                    b.ffn(L, 1)
                    b.rmsnorm(L, 3)
                    b.ffn(L, 2)
                    b.rmsnorm(L, 4)
                    b.qkv(L, 1, j)
                    b.store_h(L, "hT", j)
                b.S.flush(final=True)
    return nc, b


def build_L3(nt=NT, attn_only=False):
    nc = bass.Bass("TRN2", target_bir_lowering=False)
    b = Builder(nc)
    decl_attn_in(b, False, attn_only)
    if not attn_only:
        b.din("gains", [128, 7, DC])
        b.din("wo1", [DC, 128, NH, 128])
        decl_ffn_w(b, 3)
        b.dout("outT", [DC, 128, NTOK])
    if attn_only:
        b.dout("attn_dbg", [NH, 128, NTOK], BF16)
    if attn_only:
        b.dram["attnD"], b.dbuf["attnD"] = b.dram["attn_dbg"], b.dbuf["attn_dbg"]
    else:
        b.dint("attnD", [NH, 128, NTOK], BF16)
    with ExitStack() as es0:
        b.alloc_psum(es0)
        with ExitStack() as es:
            b.attn1(es)
            b.S.flush(final=attn_only)
        with ExitStack() as es:
            if not attn_only:
                L = b.tok_alloc(es, need_attn=True)
                for j in range(nt):
                    b.load_h(L, "hTin", j)
                    b.outproj(L, 1, j)
                    b.rmsnorm(L, 5)
                    b.ffn(L, 3)
                    b.rmsnorm(L, 6, out32=("outT", j))
                b.S.flush(final=True)
    return nc, b


def build_fused():
    nc = bass.Bass("TRN2", target_bir_lowering=False)
    b = Builder(nc, n_dma_sems=30, n_cc_sems=2)
    b.fused = True
    b.din("xT", [DC, 128, NTOK])
    b.din("gains", [128, 7, DC])
    for f in range(4):
        decl_ffn_w(b, f)
    for l in range(2):
        decl_qkv_w(b, l)
        b.din("wo%d" % l, [DC, 128, NH, 128])
    b.din("cstb", [128, CSTW])
    b.din("rel_bias", [32, 16])
    b.din("Mdil", [32, XL])
    b.din("Mmoba", [32, XL])
    b.din("pastneg", [128, 256])
    b.din("notown", [128, 256])
    b.din("sbmask", [128, 8, TS])
    b.dout("outT", [DC, 128, NTOK])
    b.dint("hT", [DC, 128, NTOK])
    b.dint("qT", [NH, 128, NTOK], BF16)
    for j in range(NT):
        b.dint("kTl%d" % j, [NH * HD, TS], BF16)
        b.dint("kTg%d" % j, [2 * NH * HD, TS], BF16)
        b.dint("vl%d" % j, [TS, D], BF16)
        b.dint("vg%d" % j, [2 * TS, D], BF16)
    b.dint("kml", [8 * 128, 8])
    b.dint("kmg", [2 * 8 * 128, 8])
    b.dint("flat", [16, XL])
    b.dint("attnD", [NH, 128, NTOK], BF16)
    with ExitStack() as es0:
        b.alloc_psum(es0)
        with ExitStack() as es:
            L = b.tok_alloc(es)
            km = b.sb(es, "km", [128, 8, 8], F32, dma=True)
            nxt = b.load_h(L, "xT", 0)
            for j in range(NT):
                L["h"] = nxt
                if j + 1 < NT:
                    nxt = b.load_h(L, "xT", j + 1)
                b.rmsnorm(L, 0)
                b.ffn(L, 0)
                b.rmsnorm(L, 1)
                b.qkv(L, 0, j, km=km)
                b.store_h(L, "hT", j)
                b.exchange_tile(j, L, defer=True)
            b.flush_cc(L)
            b.ts("dve", km.t[:], km.b, km.t[:], km.b, 1.0 / 256.0, None, ALU.mult)
            b.store("sp", b.dram["kml"].rearrange("(h p) n -> p h n", p=128), "kml", km.t[:], km)
            b.allgather("kml", "kmg")
            b.S.flush()
        with ExitStack() as es:
            b.attn0(es)
            b.S.flush()
        with ExitStack() as es:
            L = b.tok_alloc(es, need_attn=True)
            nxt = b.load_h(L, "hT", 0)
            for j in range(NT):
                L["h"] = nxt
                if j + 1 < NT:
                    nxt = b.load_h(L, "hT", j + 1)
                b.outproj(L, 0, j)
                b.rmsnorm(L, 2)
                b.ffn(L, 1)
                b.rmsnorm(L, 3)
                b.ffn(L, 2)
                b.rmsnorm(L, 4)
                b.qkv(L, 1, j)
                b.store_h(L, "hT", j)
                b.exchange_tile(j, L, defer=True)
            b.flush_cc(L)
            b.S.flush()
        with ExitStack() as es:
            b.attn1(es)
            b.S.flush()
        with ExitStack() as es:
            L = b.tok_alloc(es, need_attn=True)
            nxt = b.load_h(L, "hT", 0)
            for j in range(NT):
                L["h"] = nxt
                if j + 1 < NT:
                    nxt = b.load_h(L, "hT", j + 1)
                b.outproj(L, 1, j)
                b.rmsnorm(L, 5)
                b.ffn(L, 3)
                b.rmsnorm(L, 6, out32=("outT", j))
            b.S.flush(final=True)
    return nc, b


def rel_bucket_np(d):
    d = np.maximum(d, 0)
    df = np.maximum(d, 1).astype(np.float32)
    large = 16 + (np.log(df / np.float32(16)) / np.float32(math.log(2048 / 16)) * np.float32(16)).astype(np.int32)
    large = np.minimum(large, 31)
    return np.where(d < 16, d, large)


def pos_tables(r):
    x = np.arange(XL)
    d = x - 1023 + r * 512
    bk = rel_bucket_np(d)
    mult = ((d <= 128).astype(np.float32) + ((d % 4 == 0) & (d <= 512)) + ((d % 16 == 0) & (d <= 2048)))
    valid = d >= 0
    Mdil = np.zeros((32, XL), np.float32)
    Mmoba = np.zeros((32, XL), np.float32)
    Mdil[bk[valid], x[valid]] = mult[valid]
    Mmoba[bk[valid], x[valid]] = 1.0
    pastneg = np.zeros((128, 4, 4, 16), np.float32)
    notown = np.ones((128, 4, 4, 16), np.float32)
    p = np.arange(128)
    for j in range(4):
        for sub in range(4):
            pos = (2 * j + r) * 512 + sub * 128 + p
            own = pos // 256
            n = np.arange(16)[None, :]
            pastneg[:, j, sub, :] = np.where(n < own[:, None], 0.0, -1e30)
            notown[:, j, sub, :] = np.where(n == own[:, None], 0.0, 1.0)
    s = np.arange(128)[:, None]
    t = np.arange(512)[None, :]
    sbmask = np.zeros((128, 8, 512), np.float32)
    tt_ = np.arange(512)
    for m in range(8):
        c = r * 512 - m * 128
        ks = tt_ + c
        ok = ks <= 127
        sbmask[np.maximum(ks[ok], 0), m, tt_[ok]] = -BIG
    return dict(Mdil=Mdil, Mmoba=Mmoba, pastneg=pastneg.reshape(128, 256),
                notown=notown.reshape(128, 256), sbmask=sbmask)


def const_tables():
    cst = np.zeros((128, CSTW), np.float32)
    cst[:, 0:128] = 1.0
    k = np.arange(128)[:, None]
    m = np.arange(128)[None, :]
    cst[:, 128:256] = (k >= m)
    sel = np.zeros((16, 16, 128), np.float32)
    for n in range(16):
        sel[n, n, :] = -BIG
    cst[0:16, 256:256 + 2048] = sel.reshape(16, 16 * 128)
    cst[:, 256 + 2048:] = (k <= m)
    return cst


def lay_unit(W, ncol):
    K, N = W.shape
    return np.ascontiguousarray(W.reshape(K // 128, 128, N // ncol, ncol).transpose(2, 1, 0, 3))


def prep_weights(inp):
    w = {}
    for i in range(2):
        for k in range(2):
            f = 2 * i + k
            w["wg%d" % f] = lay_unit(inp["ffn_w_gate"][i, k], 128)
            w["wu%d" % f] = lay_unit(inp["ffn_w_up"][i, k], 128)
            w["wd%d" % f] = lay_unit(inp["ffn_w_down"][i, k], 128)
    for l, (wq, wo) in enumerate([(inp["w_qkv_even"][0], inp["w_out_even"][0]),
                                  (inp["w_qkv_odd"][0], inp["w_out_odd"][0])]):
        w["wqk%d" % l] = lay_unit(wq[:, :2 * D], 128)
        w["wv%d" % l] = lay_unit(wq[:, 2 * D:], 256)
        w["wo%d" % l] = lay_unit(wo, 128)
    g = np.concatenate([inp["ln_gains"].reshape(6, D), inp["final_gain"][None]], 0)
    w["gains"] = np.ascontiguousarray(g.reshape(7, DC, 128).transpose(2, 0, 1))
    return w


def tok_index(r):
    return np.concatenate([np.arange((2 * j + r) * 512, (2 * j + r + 1) * 512) for j in range(4)])


_CACHE = {}


def _prog(name, fn):
    if name not in _CACHE:
        _CACHE[name] = fn()[0]
    return _CACHE[name]


def kernel(x, ln_gains, ffn_w_gate, ffn_w_up, ffn_w_down, w_qkv_even, w_out_even,
           w_qkv_odd, w_out_odd, rel_bias, final_gain):
    inp = dict(x=x, ln_gains=ln_gains, ffn_w_gate=ffn_w_gate, ffn_w_up=ffn_w_up,
               ffn_w_down=ffn_w_down, w_qkv_even=w_qkv_even, w_out_even=w_out_even,
               w_qkv_odd=w_qkv_odd, w_out_odd=w_out_odd, rel_bias=rel_bias, final_gain=final_gain)
    inp = {k: np.asarray(v, np.float32) for k, v in inp.items()}
    W = prep_weights(inp)
    cores = list(range(8))
    cst = const_tables()
    ptab = [pos_tables(r) for r in range(2)]

    def pick(names):
        return {n: W[n] for n in names}

    maps = []
    for c in cores:
        bb, r = c // 2, c % 2
        xs = inp["x"][bb][tok_index(r)]
        m = dict(W)
        m.update(xT=np.ascontiguousarray(xs.T.reshape(DC, 128, NTOK)), cstb=cst, rel_bias=inp["rel_bias"],
                 Mdil=ptab[r]["Mdil"], Mmoba=ptab[r]["Mmoba"], pastneg=ptab[r]["pastneg"],
                 notown=ptab[r]["notown"], sbmask=ptab[r]["sbmask"])
        maps.append(m)
    res = run_bass_kernel_spmd(_prog("fused", build_fused), maps, core_ids=cores).results
    out = np.empty((4, SEQ, D), np.float32)
    for c in cores:
        bb, r = c // 2, c % 2
        oT = np.asarray(res[c]["outT"]).reshape(D, NTOK)
        out[bb][tok_index(r)] = oT.T
    return out
```
